# Optimizing a Trainium2 kernel written in Bass

```python
import jax, jax.numpy as jnp
from jax import lax
import numpy as np

D_MODEL = 1024
BATCH = 16
SEQ = 2048
DEPTH = 2

CHUNK = 64
N_MIXERS = 2
N_A_LAYERS = (DEPTH + N_MIXERS - 1) // N_MIXERS
N_B_LAYERS = DEPTH // N_MIXERS

GMLP_BLOCK = 128
GMLP_GROUPS = 8
GMLP_GROUP_DIM = D_MODEL // GMLP_GROUPS

N_HEADS = 16
HEAD_DIM = D_MODEL // N_HEADS
Q_BLOCK = 128

D_FF = -(-8 * D_MODEL // (3 * 256)) * 256

EPS = 1e-6

kernel_name = "hybrid_gmlp_fox_streaming_encoder"


def rms_norm(x, g):
    xf = x.astype(jnp.float32)
    y = xf * lax.rsqrt(jnp.mean(xf * xf, axis=-1, keepdims=True) + EPS)
    return (y * g.astype(jnp.float32)).astype(x.dtype)


def gmlp_mixer(h, w_in, ln_g, ln_b, w_s, b_s, w_out):
    bsz, seq, _ = h.shape
    z = jax.nn.gelu(h @ w_in, approximate=False)
    u, v = jnp.split(z, 2, axis=-1)
    vf = v.astype(jnp.float32)
    mu = jnp.mean(vf, axis=-1, keepdims=True)
    var = jnp.mean(jnp.square(vf - mu), axis=-1, keepdims=True)
    vn = ((vf - mu) * lax.rsqrt(var + EPS) * ln_g.astype(jnp.float32)
          + ln_b.astype(jnp.float32)).astype(h.dtype)
    nblk = seq // GMLP_BLOCK
    vn = vn.reshape(bsz, nblk, GMLP_BLOCK, GMLP_GROUPS, GMLP_GROUP_DIM)
    chunk_id = jnp.arange(GMLP_BLOCK) // CHUNK
    mask = chunk_id[None, :] <= chunk_id[:, None]
    ws = jnp.where(mask[None], w_s, 0)
    mixed = jnp.einsum('gts,bnsgc->bntgc', ws, vn) + b_s.T[None, None, :, :, None]
    gated = u * mixed.reshape(bsz, seq, D_MODEL).astype(u.dtype)
    return gated @ w_out


def fox_mixer(h, w_in, f_bias, q_g, k_g, w_out):
    bsz, seq, _ = h.shape
    proj = h @ w_in
    q, k, v, g, f_logit = jnp.split(
        proj, [D_MODEL, 2 * D_MODEL, 3 * D_MODEL, 4 * D_MODEL], axis=-1)
    q = rms_norm(q.reshape(bsz, seq, N_HEADS, HEAD_DIM), q_g)
    k = rms_norm(k.reshape(bsz, seq, N_HEADS, HEAD_DIM), k_g)
    v = v.reshape(bsz, seq, N_HEADS, HEAD_DIM)
    log_f = jax.nn.log_sigmoid((f_logit + f_bias).astype(jnp.float32))
    cum = jnp.cumsum(log_f, axis=1).transpose(0, 2, 1)
    scale = HEAD_DIM ** -0.5
    local = jnp.arange(Q_BLOCK)
    outs = []
    for i in range(seq // Q_BLOCK):
        q0 = i * Q_BLOCK
        kl = q0 + Q_BLOCK
        s = jnp.einsum('bqhd,bkhd->bhqk', q[:, q0:kl], k[:, :kl],
                       preferred_element_type=jnp.float32) * scale
        s = s + cum[:, :, q0:kl, None] - cum[:, :, None, :kl]
        causal = (q0 + local)[:, None] >= jnp.arange(kl)[None, :]
        s = jnp.where(causal[None, None], s, -jnp.inf)
        p = jax.nn.softmax(s, axis=-1).astype(v.dtype)
        outs.append(jnp.einsum('bhqk,bkhd->bqhd', p, v[:, :kl]))
    o = jnp.concatenate(outs, axis=1)
    o = o.reshape(bsz, seq, D_MODEL) * jax.nn.sigmoid(g)
    return o @ w_out


def swiglu(h, w_gate, w_up, w_down):
    return (jax.nn.silu(h @ w_gate) * (h @ w_up)) @ w_down


def setup_inputs(seed: int = 0) -> dict:
    key = jax.random.key(seed)
    ks = jax.random.split(key, 20)
    f32 = jnp.float32
    D, H, F = D_MODEL, N_HEADS, D_FF
    nrm = lambda k, shape: jax.random.normal(k, shape, f32)
    return {
        "x": nrm(ks[0], (BATCH, SEQ, D)),
        "norm_mix_g": 1.0 + 0.02 * nrm(ks[1], (DEPTH, D)),
        "norm_ffn_g": 1.0 + 0.02 * nrm(ks[2], (DEPTH, D)),
        "a_w_in": nrm(ks[3], (N_A_LAYERS, D, 2 * D)) * D ** -0.5,
        "a_ln_g": 1.0 + 0.02 * nrm(ks[4], (N_A_LAYERS, D)),
        "a_ln_b": 0.02 * nrm(ks[5], (N_A_LAYERS, D)),
        "a_w_s": nrm(ks[6], (N_A_LAYERS, GMLP_GROUPS, GMLP_BLOCK, GMLP_BLOCK)) * GMLP_BLOCK ** -0.5,
        "a_b_s": 1.0 + 0.1 * nrm(ks[7], (N_A_LAYERS, GMLP_GROUPS, GMLP_BLOCK)),
        "a_w_out": nrm(ks[8], (N_A_LAYERS, D, D)) * D ** -0.5,
        "b_w_in": nrm(ks[9], (N_B_LAYERS, D, 4 * D + H)) * D ** -0.5,
        "b_f_bias": 3.0 + 0.5 * nrm(ks[10], (N_B_LAYERS, H)),
        "b_q_norm_g": 1.0 + 0.02 * nrm(ks[11], (N_B_LAYERS, HEAD_DIM)),
        "b_k_norm_g": 1.0 + 0.02 * nrm(ks[12], (N_B_LAYERS, HEAD_DIM)),
        "b_w_out": nrm(ks[13], (N_B_LAYERS, D, D)) * D ** -0.5,
        "ffn_w_gate": nrm(ks[14], (DEPTH, D, F)) * D ** -0.5,
        "ffn_w_up": nrm(ks[15], (DEPTH, D, F)) * D ** -0.5,
        "ffn_w_down": nrm(ks[16], (DEPTH, F, D)) * F ** -0.5,
    }


def reference(x, norm_mix_g, norm_ffn_g, a_w_in, a_ln_g, a_ln_b, a_w_s, a_b_s, a_w_out,
              b_w_in, b_f_bias, b_q_norm_g, b_k_norm_g, b_w_out,
              ffn_w_gate, ffn_w_up, ffn_w_down):
    h = x
    for layer in range(DEPTH):
        j = layer // N_MIXERS
        hn = rms_norm(h, norm_mix_g[layer])
        if layer % N_MIXERS == 0:
            h = h + gmlp_mixer(hn, a_w_in[j], a_ln_g[j], a_ln_b[j], a_w_s[j], a_b_s[j], a_w_out[j])
        else:
            h = h + fox_mixer(hn, b_w_in[j], b_f_bias[j], b_q_norm_g[j], b_k_norm_g[j], b_w_out[j])
        h = h + swiglu(rms_norm(h, norm_ffn_g[layer]), ffn_w_gate[layer], ffn_w_up[layer], ffn_w_down[layer])
    return h
```

```python
import contextlib
import numpy as np
import concourse.bass as bass
import concourse.mybir as mybir
from concourse.bass_utils import run_bass_kernel_spmd

F32 = mybir.dt.float32
BF16 = mybir.dt.bfloat16
AF = mybir.ActivationFunctionType
ALU = mybir.AluOpType

ENGS = ("pe", "act", "dve", "pool", "sp")
N_DMA_SEMS = 4
SEQ = 2048
D = 1024
DFF = 2816
NH = 16
EPS = 1e-6
N_CORES = 8


class _Op:
    __slots__ = ("eng", "fn", "deps", "is_dma", "needs_inc", "count", "dsem")

    def __init__(self, eng, fn, is_dma):
        self.eng = eng
        self.fn = fn
        self.deps = []
        self.is_dma = is_dma
        self.needs_inc = False
        self.count = None
        self.dsem = None


class Sched:
    def __init__(self):
        self.streams = {e: [] for e in ENGS}
        self.last_writer = {}
        self.readers = {}
        self.dma_rr = {e: 0 for e in ENGS}
        self.dma_last = {}

    def op(self, eng, fn, reads=(), writes=(), dma=False):
        o = _Op(eng, fn, dma)
        deps = []
        lw = self.last_writer
        rd = self.readers
        for r in reads:
            w = lw.get(r)
            if w is not None:
                deps.append(w)
        for w_ in writes:
            w = lw.get(w_)
            if w is not None:
                deps.append(w)
            x = rd.get(w_)
            if x:
                deps.extend(x)
        if dma:
            slot = self.dma_rr[eng] % N_DMA_SEMS
            self.dma_rr[eng] += 1
            o.dsem = (eng, slot)
            prev = self.dma_last.get(o.dsem)
            if prev is not None:
                deps.append(prev)
            self.dma_last[o.dsem] = o
            o.needs_inc = True
        seen = set()
        for d in deps:
            if d is o or id(d) in seen:
                continue
            seen.add(id(d))
            if d.eng == "pe" and eng == "pe" and not d.is_dma and not dma:
                continue
            d.needs_inc = True
            o.deps.append(d)
        for r in reads:
            l = rd.get(r)
            if l is None:
                rd[r] = [o]
            else:
                l.append(o)
        for w_ in writes:
            lw[w_] = o
            rd[w_] = []
        self.streams[eng].append(o)
        return o

    def emit(self, nc):
        with contextlib.ExitStack() as st:
            sems = {}
            for e in ENGS:
                sems[e] = st.enter_context(nc.semaphore("s_" + e))
                for k in range(N_DMA_SEMS):
                    sems[(e, k)] = st.enter_context(nc.semaphore("d_%s%d" % (e, k)))
            for e in ENGS:
                c = 0
                dc = [0] * N_DMA_SEMS
                for o in self.streams[e]:
                    if o.is_dma:
                        dc[o.dsem[1]] += 16
                        o.count = dc[o.dsem[1]]
                    elif o.needs_inc:
                        c += 1
                        o.count = c
            block = st.enter_context(nc.Block())
            streams = self.streams

            def run(engname, engobj):
                waited = {}
                for o in streams[engname]:
                    need = {}
                    for d in o.deps:
                        key = d.dsem if d.is_dma else d.eng
                        if d.count > need.get(key, 0):
                            need[key] = d.count
                    for key, v in need.items():
                        if v > waited.get(key, 0):
                            engobj.wait_ge(sems[key], v)
                            waited[key] = v
                    ins = o.fn(engobj)
                    if o.is_dma:
                        ins.then_inc(sems[o.dsem], 16)
                    elif o.needs_inc:
                        ins.then_inc(sems[engname], 1)
                last = {}
                for o in streams[engname]:
                    if o.is_dma:
                        last[o.dsem] = o.count
                for key, v in last.items():
                    if v > waited.get(key, 0):
                        engobj.wait_ge(sems[key], v)

            @block.tensor
            def _(e):
                run("pe", e)

            @block.scalar
            def _(e):
                run("act", e)

            @block.vector
            def _(e):
                run("dve", e)

            @block.gpsimd
            def _(e):
                run("pool", e)

            @block.sync
            def _(e):
                run("sp", e)


GR = 256


class Arena:
    def __init__(self, name, t, dsize, nel):
        self.name = name
        self.t = t
        self.dsize = dsize
        self.nel = nel

    def keys(self, off, n):
        b0 = (off * self.dsize) // GR
        b1 = ((off + n) * self.dsize - 1) // GR
        nm = self.name
        return [(nm, b) for b in range(b0, b1 + 1)]


class V:
    def __init__(self, ar, off, shape):
        self.ar = ar
        self.off = off
        self.shape = list(shape)
        n = 1
        for s in shape:
            n *= s
        self.n = n
        assert off + n <= ar.nel, (ar.name, off, n, ar.nel)
        self._k = None

    def ap(self, p0=0, p1=128):
        a = self.ar.t[p0:p1, self.off:self.off + self.n]
        if len(self.shape) == 2:
            a = a.rearrange("p (a b) -> p a b", b=self.shape[1])
        elif len(self.shape) == 3:
            a = a.rearrange("p (a b c) -> p a b c", b=self.shape[1], c=self.shape[2])
        return a

    @property
    def k(self):
        if self._k is None:
            self._k = self.ar.keys(self.off, self.n)
        return self._k

    def __getitem__(self, i):
        sub = self.shape[1:]
        n = 1
        for s in sub:
            n *= s
        return V(self.ar, self.off + i * n, sub if sub else [1])

    def cols(self, a, b):
        assert len(self.shape) == 1
        return V(self.ar, self.off + a, [b - a])


class Bump:
    def __init__(self, ar, off, limit):
        self.ar = ar
        self.off = off
        self.limit = limit

    def __call__(self, shape):
        n = 1
        for s in shape:
            n *= s
        al = GR // self.ar.dsize
        off = (self.off + al - 1) // al * al
        v = V(self.ar, off, shape)
        self.off = off + n
        assert self.off <= self.limit, (self.ar.name, self.off, self.limit)
        return v


F_ELEMS = 24320
B_ELEMS = 56320
FFN_GROUPS = [(0, 4), (4, 4), (8, 4), (12, 4), (16, 3), (19, 3)]


def build_program(nseq=2, stages=("a", "f0", "b", "f1"), debug_out=False):
    nc = bass.Bass("TRN2", target_bir_lowering=False)
    dr = {}

    def din(name, shape):
        dr[name] = nc.dram_tensor(name, list(shape), F32, kind="ExternalInput").ap()
        return dr[name]

    x_d = din("x", [nseq, SEQ, D])
    gl_d = din("gl", [128, 32])
    alng_d = din("alng", [128, 8])
    alnb_d = din("alnb", [1, D])
    aws_d = din("aws", [8, 128, 128])
    abs_d = din("abs", [1, 1024])
    fb_d = din("fb", [16, 1])
    gq_d = din("gq", [128, 1])
    gk_d = din("gk", [128, 1])
    awin_d = din("a_w_in", [D, 2 * D])
    awout_d = din("a_w_out", [D, D])
    bwin_d = din("b_w_in", [D, 4 * D + NH])
    bwout_d = din("b_w_out", [D, D])
    wg_d = din("ffn_w_gate", [2, D, DFF])
    wu_d = din("ffn_w_up", [2, D, DFF])
    wd_d = din("ffn_w_down", [2, DFF, D])
    out_d = nc.dram_tensor("out", [nseq, SEQ, D], F32, kind="ExternalOutput").ap()

    S = Sched()
    st = contextlib.ExitStack()
    with st:
        fa_t = st.enter_context(nc.sbuf_tensor("fa", [128, F_ELEMS], F32))
        ba_t = st.enter_context(nc.sbuf_tensor("ba", [128, B_ELEMS], BF16))
        FA = Arena("f", fa_t, 4, F_ELEMS)
        BA = Arena("b", ba_t, 2, B_ELEMS)
        psf = [st.enter_context(nc.psum_tensor("psf%d" % i, [128, 512], F32)) for i in range(7)]
        psb = [st.enter_context(nc.psum_tensor("psb%d" % i, [128, 1024], BF16)) for i in range(1)]
        PK = [[("ps", i)] for i in range(7)]
        PBK = [[("psb", i)] for i in range(1)]

        fb_ = Bump(FA, 0, F_ELEMS)
        h = fb_([16, 1024])
        Bias = fb_([8, 128])
        gl = fb_([32])
        alng = fb_([8])
        ms = fb_([16])
        mse = fb_([16])
        rstd = fb_([16])
        mh = fb_([64])
        epsc = fb_([1])
        identf = fb_([128])
        gqs = fb_([1])
        gks = fb_([1])
        negfb = fb_([1])
        onesrow = fb_([64])
        negc = fb_([16, 16])
        st6 = fb_([2, 6])
        mv = fb_([2])
        ve = fb_([1])
        rs = fb_([1])
        FS0 = (fb_.off + 63) // 64 * 64
        bb_ = Bump(BA, 0, B_ELEMS)
        hnT = bb_([8, SEQ])
        identb = bb_([128])
        BD = bb_([128])
        wsT = bb_([8, 128])
        hs = [bb_([1024]), bb_([1024])]
        junk = bb_([1024])
        fillsrc = bb_([512])
        BS0 = (bb_.off + 127) // 128 * 128

        bank_rr = [0]

        def nbank():
            b = bank_rr[0] % 6
            bank_rr[0] += 1
            return b

        def mm(bank, out_ap, lhsT, rhs, start, stop, reads):
            S.op("pe", lambda e: e.matmul(out_ap, lhsT=lhsT, rhs=rhs, start=start, stop=stop),
                 reads=reads, writes=PK[bank])

        def wdma(out_v, out_ap, in_ap):
            S.op("pool", lambda e: e.dma_start(out=out_ap, in_=in_ap), writes=out_v.k, dma=True)

        def sdma(v, ap_out, ap_in):
            S.op("sp", lambda e: e.dma_start(out=ap_out, in_=ap_in), writes=v.k, dma=True)

        sdma(gl, gl.ap(), gl_d)
        sdma(alng, alng.ap(), alng_d)
        sdma(gqs, gqs.ap(), gq_d)
        sdma(gks, gks.ap(), gk_d)
        sdma(negfb, negfb.ap(0, 16), fb_d)
        S.op("pool", lambda e: e.memset(mh.ap(), -0.5), writes=mh.k)
        S.op("pool", lambda e: e.memset(epsc.ap(), EPS), writes=epsc.k)
        S.op("pool", lambda e: e.memset(identf.ap(), 0.0), writes=identf.k)
        S.op("pool", lambda e: e.affine_select(out=identf.ap(), in_=identf.ap(), pattern=[[-1, 128]],
                                               compare_op=ALU.not_equal, fill=1.0, base=0, channel_multiplier=1),
             reads=identf.k, writes=identf.k)
        S.op("dve", lambda e: e.tensor_copy(out=identb.ap(), in_=identf.ap()), reads=identf.k, writes=identb.k)
        S.op("pool", lambda e: e.memset(BD.ap(), 0.0), writes=BD.k)
        S.op("pool", lambda e: e.memset(BD.ap(0, 64)[:, 0:64], 1.0 / 64), reads=BD.k, writes=BD.k)
        S.op("pool", lambda e: e.memset(BD.ap(64, 128)[:, 64:128], 1.0 / 64), reads=BD.k, writes=BD.k)
        S.op("pool", lambda e: e.memset(onesrow.ap(), 1.0), writes=onesrow.k)
        S.op("pool", lambda e: e.memset(fillsrc.ap(), 0.5), writes=fillsrc.k)
        S.op("dve", lambda e: e.tensor_scalar(out=gqs.ap(), in0=gqs.ap(), scalar1=0.125, scalar2=None, op0=ALU.mult),
             reads=gqs.k, writes=gqs.k)
        S.op("dve", lambda e: e.tensor_scalar(out=negfb.ap(0, 16), in0=negfb.ap(0, 16), scalar1=-1.0, scalar2=None,
                                              op0=ALU.mult), reads=negfb.k, writes=negfb.k)
        if "a" in stages or "ia" in stages:
            fi = Bump(FA, FS0, F_ELEMS)
            wsn = fi([8, 128])
            wsTf = fi([8, 128])
            Bmat = fi([1024])
            bsb = fi([8, 128])
            sdma(wsn, wsn.ap(), aws_d.rearrange("g t s -> t g s"))
            sdma(Bmat, Bmat.ap(), alnb_d.partition_broadcast(128))
            sdma(bsb, bsb.ap().rearrange("p a b -> p (a b)"), abs_d.partition_broadcast(128))
            S.op("pool", lambda e: e.memset(wsn.ap(0, 64)[:, :, 64:128], 0.0), reads=wsn.k, writes=wsn.k)
            for g in range(8):
                b = nbank()
                S.op("pe", lambda e, g=g, b=b: e.transpose(out=psf[b][:, 0:128], in_=wsn.ap()[:, g, :],
                                                          identity=identf.ap()),
                     reads=wsn.k + identf.k, writes=PK[b])
                S.op("dve", lambda e, g=g, b=b: e.tensor_copy(out=wsTf.ap()[:, g, :], in_=psf[b][:, 0:128]),
                     reads=PK[b], writes=wsTf[g].k)
                S.op("pool", lambda e, g=g: e.tensor_copy(out=wsT.ap()[:, g, :], in_=wsTf.ap()[:, g, :]),
                     reads=wsTf[g].k, writes=wsT[g].k)
                b2 = nbank()
                S.op("pe", lambda e, g=g, b2=b2: e.matmul(psf[b2][:, 0:128], lhsT=Bmat.ap()[:, g * 128:(g + 1) * 128],
                                                          rhs=wsTf.ap()[:, g, :], start=True, stop=True),
                     reads=Bmat.k + wsTf[g].k, writes=PK[b2])
                S.op("dve", lambda e, g=g, b2=b2: e.tensor_tensor(out=Bias.ap()[:, g, :], in0=psf[b2][:, 0:128],
                                                                  in1=bsb.ap()[:, g, :], op=ALU.add),
                     reads=PK[b2] + bsb.k, writes=Bias[g].k)

        def norm_stage(j):
            for i in range(4):
                norm_tile(j, i)

        def norm_tile(j, i):
            if True:
                S.op("pool", lambda e, i=i: e.memset(ms.ap()[:, 4 * i:4 * i + 4], 0.0), writes=ms.k)
                for n in range(4 * i, 4 * i + 4):
                    S.op("act", lambda e, n=n: e.activation(out=junk.ap(), in_=h.ap()[:, n, :], func=AF.Square,
                                                            scale=1.0 / 32, accum_out=ms.ap()[:, n:n + 1]),
                         reads=h[n].k, writes=junk.k + ms.k)
                S.op("dve", lambda e, i=i: e.tensor_scalar(out=mse.ap()[:, 4 * i:4 * i + 4], in0=ms.ap()[:, 4 * i:4 * i + 4],
                                                           scalar1=EPS, scalar2=None, op0=ALU.add),
                     reads=ms.k, writes=mse.k)
                S.op("act", lambda e, i=i: e.activation(out=mse.ap()[:, 4 * i:4 * i + 4], in_=mse.ap()[:, 4 * i:4 * i + 4],
                                                        func=AF.Sqrt), reads=mse.k, writes=mse.k)
                S.op("dve", lambda e, i=i: e.reciprocal(out=rstd.ap()[:, 4 * i:4 * i + 4], in_=mse.ap()[:, 4 * i:4 * i + 4]),
                     reads=mse.k, writes=rstd.k)
                for n in range(4 * i, 4 * i + 4):
                    hb = hs[n % 2]
                    pb = 0
                    S.op("dve", lambda e, n=n, hb=hb: e.tensor_scalar(out=hb.ap(), in0=h.ap()[:, n, :],
                                                                       scalar1=rstd.ap()[:, n:n + 1], scalar2=None,
                                                                       op0=ALU.mult),
                         reads=h[n].k + rstd.k, writes=hb.k)
                    for c in range(8):
                        S.op("pe", lambda e, c=c, hb=hb, pb=pb: e.transpose(out=psb[pb][:, c * 128:(c + 1) * 128],
                                                                           in_=hb.ap()[:, c * 128:(c + 1) * 128],
                                                                           identity=identb.ap()),
                             reads=hb.k + identb.k, writes=PBK[pb])
                    wk = []
                    for c in range(8):
                        wk += hnT[c].cols(n * 128, (n + 1) * 128).k
                    S.op("dve", lambda e, n=n, pb=pb: e.tensor_tensor(
                        out=hnT.ap()[:, :, n * 128:(n + 1) * 128],
                        in0=psb[pb][:, :].rearrange("p (c t) -> p c t", t=128),
                        in1=gl.ap()[:, j * 8:(j + 1) * 8].unsqueeze(2).to_broadcast([128, 8, 128]),
                        op=ALU.mult), reads=PBK[pb] + gl.k, writes=wk)

        def hn_tile(kc, i):
            return hnT[kc].cols(i * 512, (i + 1) * 512)

        def hn_sub(kc, n):
            return hnT[kc].cols(n * 128, (n + 1) * 128)

        def resid_add(bank, n, nb):
            hv = h[n].cols(nb * 512, (nb + 1) * 512)
            S.op("dve", lambda e: e.tensor_tensor(out=hv.ap(), in0=psf[bank][:, :], in1=hv.ap(), op=ALU.add),
                 reads=PK[bank] + hv.k, writes=hv.k)

        def gmlp_stage(on_final):
            bb = Bump(BA, BS0, B_ELEMS)
            winU = bb([8, 1024])
            winV = bb([8, 1024])
            wout = bb([8, 1024])
            uT = bb([8, 512])
            vn = bb([4, 1024])
            fs = Bump(FA, FS0, F_ELEMS)
            vg = [fs([1024]), fs([1024])]
            sptmp = [fs([512]), fs([512])]
            lnst = [(fs([2, 6]), fs([2]), fs([1]), fs([1])) for _ in range(2)]
            a_in = awin_d.rearrange("(kc p) n -> p kc n", p=128)
            for q4 in range(4):
                S.op("pool", lambda e, q4=q4: e.dma_start(out=winU.ap()[:, :, q4 * 256:(q4 + 1) * 256],
                                                          in_=a_in[:, :, q4 * 256:(q4 + 1) * 256]),
                     writes=[k_ for kc in range(8) for k_ in winU[kc].cols(q4 * 256, (q4 + 1) * 256).k], dma=True)
            for q2 in range(2):
                S.op("pool", lambda e, q2=q2: e.dma_start(out=winV.ap()[:, :, q2 * 512:(q2 + 1) * 512],
                                                          in_=a_in[:, :, 1024 + q2 * 512:1024 + (q2 + 1) * 512]),
                     writes=[k_ for kc in range(8) for k_ in winV[kc].cols(q2 * 512, (q2 + 1) * 512).k], dma=True)
            wdma(wout, wout.ap(), awout_d.rearrange("(kc p) n -> p kc n", p=128))
            def u_part(i):
                for oc in range(8):
                    b = nbank()
                    for kc in range(8):
                        mm(b, psf[b][:, :], winU.ap()[:, kc, oc * 128:(oc + 1) * 128], hn_tile(kc, i).ap(),
                           kc == 0, kc == 7, winU[kc].cols(oc * 128, (oc + 1) * 128).k + hn_tile(kc, i).k)
                    S.op("act", lambda e, oc=oc, b=b: e.activation(out=uT.ap()[:, oc, :], in_=psf[b][:, :], func=AF.Gelu),
                         reads=PK[b], writes=uT[oc].k)

            def v_part(i):
                pending = None
                for s in range(4):
                    n = 4 * i + s
                    vb = vg[s % 2]
                    st6b, mvb, veb, rsb = lnst[s % 2]
                    for nb in range(2):
                        b = nbank()
                        for kc in range(8):
                            mm(b, psf[b][:, :], hn_sub(kc, n).ap(), winV.ap()[:, kc, nb * 512:(nb + 1) * 512],
                               kc == 0, kc == 7, winV[kc].cols(nb * 512, (nb + 1) * 512).k + hn_sub(kc, n).k)
                        S.op("act", lambda e, nb=nb, b=b, vb=vb: e.activation(out=vb.ap()[:, nb * 512:(nb + 1) * 512],
                                                                             in_=psf[b][:, :], func=AF.Gelu),
                             reads=PK[b], writes=vb.cols(nb * 512, (nb + 1) * 512).k)
                        S.op("dve", lambda e, nb=nb, vb=vb, st6b=st6b: e.bn_stats(out=st6b.ap()[:, nb, :],
                                                                                 in_=vb.ap()[:, nb * 512:(nb + 1) * 512]),
                             reads=vb.cols(nb * 512, (nb + 1) * 512).k, writes=st6b.k)
                    S.op("dve", lambda e, st6b=st6b, mvb=mvb: e.bn_aggr(out=mvb.ap(), in_=st6b.ap().rearrange("p a b -> p (a b)")),
                         reads=st6b.k, writes=mvb.k)
                    S.op("dve", lambda e, mvb=mvb, veb=veb: e.tensor_scalar(out=veb.ap(), in0=mvb.ap()[:, 1:2], scalar1=EPS,
                                                                         scalar2=None, op0=ALU.add), reads=mvb.k, writes=veb.k)

                    def tail(s=s, vb=vb, mvb=mvb, veb=veb, rsb=rsb):
                        S.op("act", lambda e: e.activation(out=veb.ap(), in_=veb.ap(), func=AF.Sqrt), reads=veb.k, writes=veb.k)
                        S.op("dve", lambda e: e.reciprocal(out=rsb.ap(), in_=veb.ap()), reads=veb.k, writes=rsb.k)
                        S.op("dve", lambda e: e.tensor_scalar(out=vn.ap()[:, s, :], in0=vb.ap(),
                                                              scalar1=mvb.ap()[:, 0:1], scalar2=rsb.ap(),
                                                              op0=ALU.subtract, op1=ALU.mult),
                             reads=vb.k + mvb.k + rsb.k, writes=vn[s].k)
                    if pending is not None:
                        pending()
                    pending = tail
                pending()

            def sp_part(i):
                for g in range(8):
                    b = nbank()
                    for s in range(4):
                        mm(b, psf[b][:, s * 128:(s + 1) * 128], vn.ap()[:, s, g * 128:(g + 1) * 128], wsT.ap()[:, g, :],
                           True, True, vn[s].k + wsT[g].k)
                    sp_ = sptmp[g % 2]
                    S.op("dve", lambda e, g=g, b=b, sp_=sp_: e.scalar_tensor_tensor(
                        out=sp_.ap().rearrange("p (s t) -> p s t", t=128),
                        in0=psf[b][:, :].rearrange("p (s t) -> p s t", t=128),
                        scalar=alng.ap()[:, g:g + 1],
                        in1=Bias.ap()[:, g, :].unsqueeze(1).to_broadcast([128, 4, 128]),
                        op0=ALU.mult, op1=ALU.add), reads=PK[b] + alng.k + Bias[g].k, writes=sp_.k)
                    S.op("pool", lambda e, g=g, sp_=sp_: e.tensor_tensor(out=uT.ap()[:, g, :], in0=sp_.ap(),
                                                                         in1=uT.ap()[:, g, :], op=ALU.mult),
                         reads=sp_.k + uT[g].k, writes=uT[g].k)

            def out_part(i):
                for s in range(4):
                    n = 4 * i + s
                    for nb in range(2):
                        b = nbank()
                        for kc in range(8):
                            mm(b, psf[b][:, :], uT.ap()[:, kc, s * 128:(s + 1) * 128],
                               wout.ap()[:, kc, nb * 512:(nb + 1) * 512], kc == 0, kc == 7, uT[kc].k + wout[kc].k)
                        resid_add(b, n, nb)

            u_part(0)
            v_part(0)
            sp_part(0)
            for i in range(1, 4):
                v_part(i)
                out_part(i - 1)
                on_final(i - 1)
                u_part(i)
                sp_part(i)
            out_part(3)
            on_final(3)

        def ffn_stage(l, on_final):
            bb = Bump(BA, BS0, B_ELEMS)
            slots = []
            for _ in range(2):
                slots.append((bb([8, 512]), bb([8, 512]), bb([4, 1024])))
            hid = [bb([4, 512]), bb([4, 512])]
            fs = Bump(FA, FS0, F_ELEMS)
            sg = [fs([512]), fs([512])]
            g_in = wg_d[l].rearrange("(kc p) n -> p kc n", p=128)
            u_in = wu_d[l].rearrange("(kc p) n -> p kc n", p=128)
            d_in = wd_d[l].rearrange("(f p) n -> p f n", p=128)

            def load(gi):
                c0, nch = FFN_GROUPS[gi]
                wgv, wuv, wdv = slots[gi % 2]
                wdma(wgv, wgv.ap()[:, :, 0:nch * 128], g_in[:, :, c0 * 128:(c0 + nch) * 128])
                wdma(wuv, wuv.ap()[:, :, 0:nch * 128], u_in[:, :, c0 * 128:(c0 + nch) * 128])
                wdma(wdv, wdv.ap()[:, 0:nch, :], d_in[:, c0:c0 + nch, :])

            load(0)
            cnt = 0
            for gi, (c0, nch) in enumerate(FFN_GROUPS):
                if gi + 1 < len(FFN_GROUPS):
                    load(gi + 1)
                wgv, wuv, wdv = slots[gi % 2]
                for i in range(4):
                    hb = hid[cnt % 2]
                    cnt += 1
                    for fl in range(nch):
                        bg = nbank()
                        for kc in range(8):
                            mm(bg, psf[bg][:, :], wgv.ap()[:, kc, fl * 128:(fl + 1) * 128], hn_tile(kc, i).ap(),
                               kc == 0, kc == 7, wgv[kc].k + hn_tile(kc, i).k)
                        bu = nbank()
                        for kc in range(8):
                            mm(bu, psf[bu][:, :], wuv.ap()[:, kc, fl * 128:(fl + 1) * 128], hn_tile(kc, i).ap(),
                               kc == 0, kc == 7, wuv[kc].k + hn_tile(kc, i).k)
                        sgb = sg[fl % 2]
                        S.op("act", lambda e, bg=bg, sgb=sgb: e.activation(out=sgb.ap(), in_=psf[bg][:, :], func=AF.Silu),
                             reads=PK[bg], writes=sgb.k)
                        S.op("dve", lambda e, bu=bu, sgb=sgb, hb=hb, fl=fl: e.tensor_tensor(
                            out=hb.ap()[:, fl, :], in0=psf[bu][:, :], in1=sgb.ap(), op=ALU.mult),
                            reads=PK[bu] + sgb.k, writes=hb[fl].k)
                    for s in range(4):
                        n = 4 * i + s
                        for nb in range(2):
                            b = nbank()
                            for fl in range(nch):
                                mm(b, psf[b][:, :], hb.ap()[:, fl, s * 128:(s + 1) * 128],
                                   wdv.ap()[:, fl, nb * 512:(nb + 1) * 512], fl == 0, fl == nch - 1, hb[fl].k + wdv[fl].k)
                            resid_add(b, n, nb)
                    if gi == len(FFN_GROUPS) - 1:
                        on_final(i)

        def fox_stage(on_final):
            bb = Bump(BA, BS0, B_ELEMS)
            wf = bb([8, 16])
            pw = [tuple([bb([8, 128]) for _ in range(4)] + [bb([1024])]) for _ in range(2)]
            kT = [bb([2, SEQ]), bb([2, SEQ])]
            qT = [bb([2, 512]), bb([2, 512])]
            vaug = [bb([16, 2, 65]), bb([16, 2, 65])]
            PT = [bb([512]) for _ in range(4)]
            gTh = [bb([2, 512]), bb([2, 512])]
            ogT = [bb([512]), bb([512])]
            sq = [bb([512]), bb([512])]
            cq = bb([SEQ])
            fs = Bump(FA, FS0, F_ELEMS)
            rr = [fs([512]), fs([512])]
            me = [fs([512]), fs([512])]
            rec = [fs([512]), fs([512])]
            bsb2 = [fs([512]), fs([512])]
            gt1 = [fs([512]), fs([512])]
            fs2 = Bump(FA, FS0, F_ELEMS)
            esp = fs2([SEQ])
            ncum = fs2([SEQ])
            ones5 = fs2([512])

            w_in = bwin_d.rearrange("(kc p) n -> p kc n", p=128)
            wdma(wf, wf.ap(), w_in[:, :, 4096:4112])

            def loadpair(p):
                sl = pw[p % 2]
                for t in range(4):
                    wdma(sl[t], sl[t].ap(), w_in[:, :, t * 1024 + p * 128:t * 1024 + (p + 1) * 128])
                wdma(sl[4], sl[4].ap(), bwout_d[p * 128:(p + 1) * 128, :])

            loadpair(0)
            for kb_ in kT:
                S.op("pool", lambda e, kb_=kb_: e.memset(kb_.ap(64, 65), 1.0), writes=kb_.k)
            for vb_ in vaug:
                S.op("pool", lambda e, vb_=vb_: e.memset(vb_.ap()[:, :, :, 64:65], 1.0), writes=vb_.k)
            S.op("pool", lambda e: e.memset(ones5.ap(0, 16), 1.0), writes=ones5.k)

            for i in range(4):
                b = nbank()
                for kc in range(8):
                    mm(b, psf[b][0:16, :], wf.ap()[:, kc, :], hn_tile(kc, i).ap(), kc == 0, kc == 7,
                       wf.k + hn_tile(kc, i).k)
                ev = esp.cols(i * 512, (i + 1) * 512)
                S.op("act", lambda e, b=b, ev=ev: e.activation(out=ev.ap(0, 16), in_=psf[b][0:16, :], func=AF.Exp,
                                                               bias=negfb.ap(0, 16), scale=-1.0),
                     reads=PK[b] + negfb.k, writes=ev.k)
                S.op("act", lambda e, ev=ev: e.activation(out=ev.ap(0, 16), in_=ev.ap(0, 16), func=AF.Ln, bias=1.0, scale=1.0),
                     reads=ev.k, writes=ev.k)
                nv = ncum.cols(i * 512, (i + 1) * 512)
                if i == 0:
                    S.op("dve", lambda e, ev=ev, nv=nv: e.tensor_tensor_scan(out=nv.ap(0, 16), data0=ones5.ap(0, 16),
                                                                             data1=ev.ap(0, 16), initial=0.0,
                                                                             op0=ALU.mult, op1=ALU.add),
                         reads=ev.k + ones5.k, writes=nv.k)
                else:
                    pv = ncum.cols(i * 512 - 1, i * 512)
                    S.op("dve", lambda e, ev=ev, nv=nv, pv=pv: e.tensor_tensor_scan(out=nv.ap(0, 16), data0=ones5.ap(0, 16),
                                                                                    data1=ev.ap(0, 16), initial=pv.ap(0, 16),
                                                                                    op0=ALU.mult, op1=ALU.add),
                         reads=ev.k + ones5.k + pv.k, writes=nv.k)
                cv = cq.cols(i * 512, (i + 1) * 512)
                S.op("dve", lambda e, nv=nv, cv=cv: e.tensor_scalar(out=cv.ap(0, 16), in0=nv.ap(0, 16), scalar1=-1.0,
                                                                    scalar2=None, op0=ALU.mult),
                     reads=nv.k, writes=cv.k)
            bt = nbank()
            for n in range(16):
                S.op("pe", lambda e, n=n, bt=bt: e.transpose(out=psf[bt][:, n * 16:(n + 1) * 16],
                                                             in_=ncum.ap(0, 16)[:, n * 128:(n + 1) * 128],
                                                             identity=identf.ap(0, 16)[:, 0:16]),
                     reads=ncum.k + identf.k, writes=PK[bt])
            S.op("dve", lambda e, bt=bt: e.tensor_copy(out=negc.ap().rearrange("p a b -> p (a b)"), in_=psf[bt][:, 0:256]),
                 reads=PK[bt], writes=negc.k)

            obank = [4, 5]
            PROJ = 3
            tickno = [0]
            dseq = [0]
            deferred = []

            def later(n, fn):
                dseq[0] += 1
                deferred.append((tickno[0] + n, dseq[0], fn))
                deferred.sort(key=lambda t: (t[0], t[1]))

            NFILL = 0

            def filler():
                for _ in range(NFILL):
                    S.op("pe", lambda e: e.matmul(psf[6][:, :], lhsT=identb.ap(), rhs=fillsrc.ap(),
                                                  start=True, stop=True), reads=fillsrc.k + identb.k)

            def tick():
                tickno[0] += 1
                while deferred and deferred[0][0] <= tickno[0]:
                    deferred.pop(0)[2]()

            pc = me
            srr = [0]
            pt_rr = [0]

            def sbank():
                x = srr[0] % 3
                srr[0] += 1
                return x

            def qk_norm(b, which, gsc, dst, i, col0):
                sqb = sq[which]
                pcb = pc[which]
                rrb = rr[which]
                S.op("dve", lambda e: e.tensor_copy(out=pcb.ap(), in_=psf[b][:, :]), reads=PK[b], writes=pcb.k)
                S.op("pool", lambda e: e.tensor_tensor(out=sqb.ap(), in0=pcb.ap(), in1=pcb.ap(), op=ALU.mult),
                     reads=pcb.k, writes=sqb.k)

                def link2():
                    b2 = sbank()
                    mm(b2, psf[b2][:, :], BD.ap(), sqb.ap(), True, True, BD.k + sqb.k)
                    S.op("act", lambda e: e.activation(out=rrb.ap(), in_=psf[b2][:, :], func=AF.Ln, bias=epsc.ap(), scale=1.0),
                         reads=PK[b2] + epsc.k, writes=rrb.k)
                    S.op("act", lambda e: e.activation(out=rrb.ap(), in_=rrb.ap(), func=AF.Exp, scale=-0.5),
                         reads=rrb.k, writes=rrb.k)

                def link3():
                    for hh in range(2):
                        dv = dst[hh].cols(col0, col0 + 512)
                        S.op("dve", lambda e, hh=hh, dv=dv: e.scalar_tensor_tensor(
                            out=dv.ap(0, 64), in0=pcb.ap(hh * 64, (hh + 1) * 64),
                            scalar=gsc.ap(hh * 64, (hh + 1) * 64),
                            in1=rrb.ap(hh * 64, (hh + 1) * 64), op0=ALU.mult, op1=ALU.mult),
                            reads=pcb.k + rrb.k + gsc.k, writes=dv.k)

                later(3, link2)
                later(6, link3)

            wq = []
            projrr = [0]
            PROJB = [3, 6]

            def projbank():
                x = PROJB[projrr[0] % 2]
                projrr[0] += 1
                return x

            def kproj(p, i):
                wk = pw[p % 2][1]
                st_ = {}

                def part(k0):
                    if k0 == 0:
                        st_["b"] = projbank()
                    b = st_["b"]
                    for kc in (k0, k0 + 1):
                        mm(b, psf[b][:, :], wk.ap()[:, kc, :], hn_tile(kc, i).ap(), kc == 0, kc == 7, wk.k + hn_tile(kc, i).k)
                    if k0 == 6:
                        qk_norm(b, 0, gks, kT[p % 2], i, i * 512)
                for k0 in (0, 2, 4, 6):
                    wq.append(lambda k0=k0: part(k0))

            def vproj(p, i):
                wv = pw[p % 2][2]
                vb = vaug[p % 2]
                st_ = {}

                def part(s_):
                    if s_ == 0:
                        st_["b"] = projbank()
                    b = st_["b"]
                    n = 4 * i + s_
                    for kc in range(8):
                        mm(b, psf[b][:, s_ * 128:(s_ + 1) * 128], hn_sub(kc, n).ap(), wv.ap()[:, kc, :],
                           kc == 0, kc == 7, wv.k + hn_sub(kc, n).k)
                    if s_ == 3:
                        wkeys = []
                        for t_ in range(4):
                            wkeys += vb[4 * i + t_].k
                        S.op("dve", lambda e: e.tensor_copy(
                            out=vb.ap()[:, 4 * i:4 * i + 4, :, 0:64],
                            in_=psf[b][:, :].rearrange("p (s h d) -> p s h d", h=2, d=64)),
                            reads=PK[b], writes=wkeys)
                for s_ in range(4):
                    wq.append(lambda s_=s_: part(s_))

            def qproj(p, i):
                wq_ = pw[p % 2][0]
                qb = qT[i % 2]
                st_ = {}

                def part(k0):
                    if k0 == 0:
                        st_["b"] = projbank()
                        for hh in range(2):
                            hd = 2 * p + hh
                            S.op("sp", lambda e, hh=hh, hd=hd: e.dma_start(out=qb.ap(64, 65)[:, hh, :],
                                                                         in_=cq.ap(hd, hd + 1)[:, i * 512:(i + 1) * 512]),
                                 reads=cq.k, writes=qb[hh].k, dma=True)
                    b = st_["b"]
                    for kc in (k0, k0 + 1):
                        mm(b, psf[b][:, :], wq_.ap()[:, kc, :], hn_tile(kc, i).ap(), kc == 0, kc == 7,
                           wq_.k + hn_tile(kc, i).k)
                    if k0 == 6:
                        qk_norm(b, 1, gqs, qb, i, 0)
                for k0 in (0, 2, 4, 6):
                    wq.append(lambda k0=k0: part(k0))

            def gate_proj(p, i):
                wgt = pw[p % 2][3]
                st_ = {}

                def part(k0):
                    if k0 == 0:
                        st_["b"] = projbank()
                    b = st_["b"]
                    for kc in (k0, k0 + 1):
                        mm(b, psf[b][:, :], wgt.ap()[:, kc, :], hn_tile(kc, i).ap(), kc == 0, kc == 7,
                           wgt.k + hn_tile(kc, i).k)
                    if k0 == 6:
                        g1 = gt1[i % 2]
                        gth = gTh[i % 2]
                        S.op("act", lambda e: e.activation(out=g1.ap(), in_=psf[b][:, :], func=AF.Exp, scale=-1.0),
                             reads=PK[b], writes=g1.k)
                        S.op("act", lambda e: e.activation(out=g1.ap(), in_=g1.ap(), func=AF.Ln, bias=1.0, scale=1.0),
                             reads=g1.k, writes=g1.k)
                        for hh in range(2):
                            S.op("act", lambda e, hh=hh: e.activation(out=gth.ap(0, 64)[:, hh, :],
                                                                     in_=g1.ap(hh * 64, (hh + 1) * 64), func=AF.Exp,
                                                                     scale=-1.0),
                                 reads=g1.k, writes=gth[hh].k)
                for k0 in (0, 2, 4, 6):
                    wq.append(lambda k0=k0: part(k0))

            def finish_head(p, i, hh, og, gap):
                ob = obank[hh]
                gth = gTh[i % 2]
                S.op("dve", lambda e: e.reciprocal(out=rec[hh].ap(64, 65), in_=psf[ob][64:65, :]),
                     reads=PK[ob], writes=rec[hh].k)

                def link2():
                    b = sbank()
                    mm(b, psf[b][0:64, :], onesrow.ap(64, 65), rec[hh].ap(64, 65), True, True, onesrow.k + rec[hh].k)
                    bs2 = bsb2[hh]
                    S.op("dve", lambda e: e.tensor_tensor(out=bs2.ap(0, 64), in0=psf[b][0:64, :],
                                                          in1=gth.ap(0, 64)[:, hh, :], op=ALU.mult),
                         reads=PK[b] + gth[hh].k, writes=bs2.k)
                    S.op("dve", lambda e: e.tensor_tensor(out=og.ap(hh * 64, (hh + 1) * 64), in0=psf[ob][0:64, :],
                                                          in1=bs2.ap(0, 64), op=ALU.mult),
                         reads=PK[ob] + bs2.k, writes=og.k)
                    if hh == 1:
                        later(gap, lambda: out_proj(p, i, og))

                later(gap, link2)

            def issue_S(p, i, hh, j):
                hd = 2 * p + hh
                kb = kT[p % 2]
                c0 = max(0, (j - 4 * i) * 128)
                b = sbank()
                kv = kb[hh].cols(j * 128, (j + 1) * 128)
                qv = qT[i % 2][hh].cols(c0, 512)
                mm(b, psf[b][:, c0:512], kv.ap(0, 65), qv.ap(0, 65), True, True, kv.k + qv.k)
                pt = PT[pt_rr[0] % 4]
                pt_rr[0] += 1
                S.op("act", lambda e: e.activation(
                    out=pt.ap()[:, c0:512], in_=psf[b][:, c0:512], func=AF.Exp,
                    bias=negc.ap()[:, j, hd:hd + 1], scale=1.0),
                    reads=PK[b] + negc.k, writes=pt.k)
                if j >= 4 * i:
                    S.op("pool", lambda e: e.affine_select(
                        out=pt.ap()[:, c0:c0 + 128], in_=pt.ap()[:, c0:c0 + 128], pattern=[[1, 128]],
                        compare_op=ALU.is_ge, fill=0.0, base=0, channel_multiplier=-1),
                        reads=pt.k, writes=pt.k)
                return (p, i, hh, j, c0, pt)

            def issue_PV(p, i, hh, j, c0, pt):
                ob = obank[hh]
                vb = vaug[p % 2]
                nk = 4 * i + 4
                mm(ob, psf[ob][0:65, c0:512], vb.ap()[:, j, hh, :], pt.ap()[:, c0:512], j == 0, j == nk - 1,
                   vb[j].k + pt.k)

            def out_proj(p, i, og):
                wo = pw[p % 2][4]

                def part(s_, nb):
                    n = 4 * i + s_
                    b = sbank()
                    mm(b, psf[b][:, :], og.ap()[:, s_ * 128:(s_ + 1) * 128], wo.ap()[:, nb * 512:(nb + 1) * 512],
                       True, True, og.k + wo.k)
                    resid_add(b, n, nb)
                    if s_ == 3 and nb == 1:
                        if p == 7:
                            on_final(i)
                        if i == 3 and p + 2 < 8:
                            loadpair(p + 2)
                for s_ in range(4):
                    for nb in range(2):
                        wq.append(lambda s_=s_, nb=nb: part(s_, nb))

            BUDGET = 3

            def drain(nmax):
                k = 0
                while wq and k < nmax:
                    wq.pop(0)()
                    k += 1
                return k

            loadpair(1)
            for i in range(4):
                kproj(0, i)
                vproj(0, i)
                drain(10 ** 9)
                while deferred:
                    tick()
            qproj(0, 0)
            gate_proj(0, 0)
            drain(10 ** 9)
            while deferred:
                tick()
            LOOK = 2
            pend = []

            def after_pv(p, i, hh, j):
                if j == 4 * i + 3:
                    og = ogT[i % 2]
                    d1, gap = (1, 1) if i == 3 else ((2, 2) if i == 0 else (2, 5))
                    later(d1, lambda: finish_head(p, i, hh, og, gap))

            CHUNK_AT = [20, 27, 34, 41, 48, 55, 62, 69]
            for p in range(8):
                chunks = []
                if p + 1 < 8:
                    for i in range(4):
                        chunks.append(lambda i=i, p=p: kproj(p + 1, i))
                        chunks.append(lambda i=i, p=p: vproj(p + 1, i))
                items = [(i, hh, j) for i in range(4) for hh in range(2) for j in range(4 * i + 4)]
                for idx, (i, hh, j) in enumerate(items):
                    if hh == 0 and j == 0 and i + 1 < 4:
                        qproj(p, i + 1)
                    if hh == 1 and j == 0 and i + 1 < 4:
                        gate_proj(p, i + 1)
                    if i == 3 and hh == 1 and j == 2 and p + 1 < 8:
                        qproj(p + 1, 0)
                        gate_proj(p + 1, 0)
                    if chunks and idx in CHUNK_AT:
                        chunks.pop(0)()
                    pend.append(issue_S(p, i, hh, j))
                    if len(pend) > LOOK:
                        it = pend.pop(0)
                        issue_PV(*it)
                        after_pv(it[0], it[1], it[2], it[3])
                    tick()
                    if drain(BUDGET) == 0:
                        filler()
                while chunks:
                    chunks.pop(0)()
            while pend:
                it = pend.pop(0)
                issue_PV(*it)
                after_pv(it[0], it[1], it[2], it[3])
                tick()
                drain(BUDGET)
            while deferred or wq:
                tick()
                drain(BUDGET)

        GJ = {"a": 0, "f0": 1, "b": 2, "f1": 3}
        real = [st_ for st_ in stages if st_ in GJ]

        def hkeys(i):
            hk = []
            for n in range(4 * i, 4 * i + 4):
                hk += h[n].k
            return hk

        def load_x(sq_i, i):
            S.op("sp", lambda e: e.dma_start(
                out=h.ap()[:, 4 * i:4 * i + 4, :],
                in_=x_d[sq_i, i * 512:(i + 1) * 512, :].rearrange("(n p) d -> p n d", p=128)),
                writes=hkeys(i), dma=True)

        def store_out(sq_i, i):
            S.op("sp", lambda e: e.dma_start(
                out=out_d[sq_i, i * 512:(i + 1) * 512, :].rearrange("(n p) d -> p n d", p=128),
                in_=h.ap()[:, 4 * i:4 * i + 4, :]), reads=hkeys(i), dma=True)

        for i in range(4):
            load_x(0, i)
        if real:
            norm_stage(GJ[real[0]])
        for sq_i in range(nseq):
            for si, stg in enumerate(real):
                last = si == len(real) - 1
                if not last:
                    nj = GJ[real[si + 1]]
                    cb = lambda i, nj=nj: norm_tile(nj, i)
                else:
                    def cb(i, sq_i=sq_i):
                        store_out(sq_i, i)
                        if i == 3 and sq_i + 1 < nseq:
                            for t in range(4):
                                load_x(sq_i + 1, t)
                            for t in range(4):
                                norm_tile(GJ[real[0]], t)
                if stg == "a":
                    gmlp_stage(cb)
                elif stg == "f0":
                    ffn_stage(0, cb)
                elif stg == "b":
                    fox_stage(cb)
                elif stg == "f1":
                    ffn_stage(1, cb)
            if not real:
                for i in range(4):
                    store_out(sq_i, i)
                    if sq_i + 1 < nseq:
                        load_x(sq_i + 1, i)
        with nc.allow_low_precision(reason="bf16 matmul operands by design; fp32 accumulation"):
            S.emit(nc)
    return nc


def host_layout(inputs, seq_slice):
    f = lambda a: np.ascontiguousarray(np.asarray(a, dtype=np.float32))
    gains = np.stack([np.asarray(inputs["norm_mix_g"])[0], np.asarray(inputs["norm_ffn_g"])[0],
                      np.asarray(inputs["norm_mix_g"])[1], np.asarray(inputs["norm_ffn_g"])[1]])
    gl = gains.reshape(4, 8, 128).transpose(2, 0, 1).reshape(128, 32)
    m = {
        "x": f(np.asarray(inputs["x"])[seq_slice]),
        "gl": f(gl),
        "alng": f(np.asarray(inputs["a_ln_g"])[0].reshape(8, 128).T),
        "alnb": f(np.asarray(inputs["a_ln_b"])[0].reshape(1, D)),
        "aws": f(np.asarray(inputs["a_w_s"])[0]),
        "abs": f(np.asarray(inputs["a_b_s"])[0].reshape(1, 1024)),
        "fb": f(np.asarray(inputs["b_f_bias"])[0].reshape(16, 1)),
        "gq": f(np.tile(np.asarray(inputs["b_q_norm_g"])[0], 2).reshape(128, 1)),
        "gk": f(np.tile(np.asarray(inputs["b_k_norm_g"])[0], 2).reshape(128, 1)),
        "a_w_in": f(np.asarray(inputs["a_w_in"])[0]),
        "a_w_out": f(np.asarray(inputs["a_w_out"])[0]),
        "b_w_in": f(np.asarray(inputs["b_w_in"])[0]),
        "b_w_out": f(np.asarray(inputs["b_w_out"])[0]),
        "ffn_w_gate": f(inputs["ffn_w_gate"]),
        "ffn_w_up": f(inputs["ffn_w_up"]),
        "ffn_w_down": f(inputs["ffn_w_down"]),
    }
    return m


_NC_CACHE = {}


def kernel(**inputs):
    if "full" not in _NC_CACHE:
        _NC_CACHE["full"] = build_program(nseq=2)
    nc = _NC_CACHE["full"]
    base = host_layout(inputs, slice(0, 2))
    xs = np.asarray(inputs["x"], dtype=np.float32)
    in_maps = []
    for c in range(N_CORES):
        m = dict(base)
        m["x"] = np.ascontiguousarray(xs[2 * c:2 * c + 2])
        in_maps.append(m)
    res = run_bass_kernel_spmd(nc, in_maps, core_ids=list(range(N_CORES)))
    out = np.concatenate([np.asarray(r["out"], dtype=np.float32) for r in res.results], axis=0)
    return out
```

```python
import contextlib
import numpy as np
import concourse.bass as bass
import concourse.mybir as mybir
from concourse.bass_utils import run_bass_kernel_spmd

F32 = mybir.dt.float32
BF16 = mybir.dt.bfloat16
AF = mybir.ActivationFunctionType
ALU = mybir.AluOpType

ENGS = ("pe", "act", "dve", "pool", "sp")
N_DMA_SEMS = 8
SEQ = 2048
D = 1024
DFF = 2816
NH = 16
EPS = 1e-6
N_CORES = 8


class _Op:
    __slots__ = ("eng", "fn", "deps", "is_dma", "needs_inc", "count", "dsem")

    def __init__(self, eng, fn, is_dma):
        self.eng = eng
        self.fn = fn
        self.deps = []
        self.is_dma = is_dma
        self.needs_inc = False
        self.count = None
        self.dsem = None


class Sched:
    def __init__(self):
        self.streams = {e: [] for e in ENGS}
        self.last_writer = {}
        self.readers = {}
        self.dma_rr = {e: 0 for e in ENGS}
        self.dma_last = {}

    def op(self, eng, fn, reads=(), writes=(), dma=False):
        o = _Op(eng, fn, dma)
        deps = []
        lw = self.last_writer
        rd = self.readers
        for r in reads:
            w = lw.get(r)
            if w is not None:
                deps.append(w)
        for w_ in writes:
            w = lw.get(w_)
            if w is not None:
                deps.append(w)
            x = rd.get(w_)
            if x:
                deps.extend(x)
        if dma:
            slot = self.dma_rr[eng] % N_DMA_SEMS
            self.dma_rr[eng] += 1
            o.dsem = (eng, slot)
            prev = self.dma_last.get(o.dsem)
            if prev is not None:
                deps.append(prev)
            self.dma_last[o.dsem] = o
            o.needs_inc = True
        seen = set()
        for d in deps:
            if d is o or id(d) in seen:
                continue
            seen.add(id(d))
            if d.eng == "pe" and eng == "pe" and not d.is_dma and not dma:
                continue
            d.needs_inc = True
            o.deps.append(d)
        for r in reads:
            l = rd.get(r)
            if l is None:
                rd[r] = [o]
            else:
                l.append(o)
        for w_ in writes:
            lw[w_] = o
            rd[w_] = []
        self.streams[eng].append(o)
        return o

    def emit(self, nc):
        with contextlib.ExitStack() as st:
            sems = {}
            for e in ENGS:
                sems[e] = st.enter_context(nc.semaphore("s_" + e))
                for k in range(N_DMA_SEMS):
                    sems[(e, k)] = st.enter_context(nc.semaphore("d_%s%d" % (e, k)))
            for e in ENGS:
                c = 0
                dc = [0] * N_DMA_SEMS
                for o in self.streams[e]:
                    if o.is_dma:
                        dc[o.dsem[1]] += 16
                        o.count = dc[o.dsem[1]]
                    elif o.needs_inc:
                        c += 1
                        o.count = c
            block = st.enter_context(nc.Block())
            streams = self.streams

            def run(engname, engobj):
                waited = {}
                for o in streams[engname]:
                    need = {}
                    for d in o.deps:
                        key = d.dsem if d.is_dma else d.eng
                        if d.count > need.get(key, 0):
                            need[key] = d.count
                    for key, v in need.items():
                        if v > waited.get(key, 0):
                            engobj.wait_ge(sems[key], v)
                            waited[key] = v
                    ins = o.fn(engobj)
                    if o.is_dma:
                        ins.then_inc(sems[o.dsem], 16)
                    elif o.needs_inc:
                        ins.then_inc(sems[engname], 1)
                last = {}
                for o in streams[engname]:
                    if o.is_dma:
                        last[o.dsem] = o.count
                for key, v in last.items():
                    if v > waited.get(key, 0):
                        engobj.wait_ge(sems[key], v)

            @block.tensor
            def _(e):
                run("pe", e)

            @block.scalar
            def _(e):
                run("act", e)

            @block.vector
            def _(e):
                run("dve", e)

            @block.gpsimd
            def _(e):
                run("pool", e)

            @block.sync
            def _(e):
                run("sp", e)


GR = 256


class Arena:
    def __init__(self, name, t, dsize, nel):
        self.name = name
        self.t = t
        self.dsize = dsize
        self.nel = nel

    def keys(self, off, n):
        b0 = (off * self.dsize) // GR
        b1 = ((off + n) * self.dsize - 1) // GR
        nm = self.name
        return [(nm, b) for b in range(b0, b1 + 1)]


class V:
    def __init__(self, ar, off, shape):
        self.ar = ar
        self.off = off
        self.shape = list(shape)
        n = 1
        for s in shape:
            n *= s
        self.n = n
        assert off + n <= ar.nel, (ar.name, off, n, ar.nel)
        self._k = None

    def ap(self, p0=0, p1=128):
        a = self.ar.t[p0:p1, self.off:self.off + self.n]
        if len(self.shape) == 2:
            a = a.rearrange("p (a b) -> p a b", b=self.shape[1])
        elif len(self.shape) == 3:
            a = a.rearrange("p (a b c) -> p a b c", b=self.shape[1], c=self.shape[2])
        return a

    @property
    def k(self):
        if self._k is None:
            self._k = self.ar.keys(self.off, self.n)
        return self._k

    def __getitem__(self, i):
        sub = self.shape[1:]
        n = 1
        for s in sub:
            n *= s
        return V(self.ar, self.off + i * n, sub if sub else [1])

    def cols(self, a, b):
        assert len(self.shape) == 1
        return V(self.ar, self.off + a, [b - a])


class Bump:
    def __init__(self, ar, off, limit):
        self.ar = ar
        self.off = off
        self.limit = limit

    def __call__(self, shape):
        n = 1
        for s in shape:
            n *= s
        al = GR // self.ar.dsize
        off = (self.off + al - 1) // al * al
        v = V(self.ar, off, shape)
        self.off = off + n
        assert self.off <= self.limit, (self.ar.name, self.off, self.limit)
        return v


F_ELEMS = 24320
B_ELEMS = 56320
FFN_GROUPS = [(0, 4), (4, 4), (8, 4), (12, 4), (16, 3), (19, 3)]


def build_program(nseq=2, stages=("a", "f0", "b", "f1"), debug_out=False):
    nc = bass.Bass("TRN2", target_bir_lowering=False)
    dr = {}

    def din(name, shape):
        dr[name] = nc.dram_tensor(name, list(shape), F32, kind="ExternalInput").ap()
        return dr[name]

    x_d = din("x", [nseq, SEQ, D])
    gl_d = din("gl", [128, 32])
    alng_d = din("alng", [128, 8])
    alnb_d = din("alnb", [1, D])
    aws_d = din("aws", [8, 128, 128])
    abs_d = din("abs", [1, 1024])
    fb_d = din("fb", [16, 1])
    gq_d = din("gq", [128, 1])
    gk_d = din("gk", [128, 1])
    awin_d = din("a_w_in", [D, 2 * D])
    awout_d = din("a_w_out", [D, D])
    bwin_d = din("b_w_in", [D, 4 * D + NH])
    bwout_d = din("b_w_out", [D, D])
    wg_d = din("ffn_w_gate", [2, D, DFF])
    wu_d = din("ffn_w_up", [2, D, DFF])
    wd_d = din("ffn_w_down", [2, DFF, D])
    out_d = nc.dram_tensor("out", [nseq, SEQ, D], F32, kind="ExternalOutput").ap()

    S = Sched()
    st = contextlib.ExitStack()
    with st:
        fa_t = st.enter_context(nc.sbuf_tensor("fa", [128, F_ELEMS], F32))
        ba_t = st.enter_context(nc.sbuf_tensor("ba", [128, B_ELEMS], BF16))
        FA = Arena("f", fa_t, 4, F_ELEMS)
        BA = Arena("b", ba_t, 2, B_ELEMS)
        psf = [st.enter_context(nc.psum_tensor("psf%d" % i, [128, 512], F32)) for i in range(7)]
        psb = [st.enter_context(nc.psum_tensor("psb%d" % i, [128, 1024], BF16)) for i in range(1)]
        PK = [[("ps", i)] for i in range(7)]
        PBK = [[("psb", i)] for i in range(1)]

        fb_ = Bump(FA, 0, F_ELEMS)
        h = fb_([16, 1024])
        Bias = fb_([8, 128])
        gl = fb_([32])
        alng = fb_([8])
        ms = fb_([16])
        mse = fb_([16])
        rstd = fb_([16])
        mh = fb_([64])
        epsc = fb_([1])
        identf = fb_([128])
        gqs = fb_([1])
        gks = fb_([1])
        negfb = fb_([1])
        onesrow = fb_([64])
        negc = fb_([16, 16])
        st6 = fb_([2, 6])
        mv = fb_([2])
        ve = fb_([1])
        rs = fb_([1])
        FS0 = (fb_.off + 63) // 64 * 64
        bb_ = Bump(BA, 0, B_ELEMS)
        hnT = bb_([8, SEQ])
        identb = bb_([128])
        BD = bb_([128])
        wsT = bb_([8, 128])
        hs = [bb_([1024]), bb_([1024])]
        junk = bb_([1024])
        fillsrc = bb_([512])
        BS0 = (bb_.off + 127) // 128 * 128

        bank_rr = [0]

        def nbank():
            b = bank_rr[0] % 6
            bank_rr[0] += 1
            return b

        def mm(bank, out_ap, lhsT, rhs, start, stop, reads):
            S.op("pe", lambda e: e.matmul(out_ap, lhsT=lhsT, rhs=rhs, start=start, stop=stop),
                 reads=reads, writes=PK[bank])

        def wdma(out_v, out_ap, in_ap):
            S.op("pool", lambda e: e.dma_start(out=out_ap, in_=in_ap), writes=out_v.k, dma=True)

        def sdma(v, ap_out, ap_in):
            S.op("sp", lambda e: e.dma_start(out=ap_out, in_=ap_in), writes=v.k, dma=True)

        sdma(gl, gl.ap(), gl_d)
        sdma(alng, alng.ap(), alng_d)
        sdma(gqs, gqs.ap(), gq_d)
        sdma(gks, gks.ap(), gk_d)
        sdma(negfb, negfb.ap(0, 16), fb_d)
        S.op("pool", lambda e: e.memset(mh.ap(), -0.5), writes=mh.k)
        S.op("pool", lambda e: e.memset(epsc.ap(), EPS), writes=epsc.k)
        S.op("pool", lambda e: e.memset(identf.ap(), 0.0), writes=identf.k)
        S.op("pool", lambda e: e.affine_select(out=identf.ap(), in_=identf.ap(), pattern=[[-1, 128]],
                                               compare_op=ALU.not_equal, fill=1.0, base=0, channel_multiplier=1),
             reads=identf.k, writes=identf.k)
        S.op("dve", lambda e: e.tensor_copy(out=identb.ap(), in_=identf.ap()), reads=identf.k, writes=identb.k)
        S.op("pool", lambda e: e.memset(BD.ap(), 0.0), writes=BD.k)
        S.op("pool", lambda e: e.memset(BD.ap(0, 64)[:, 0:64], 1.0 / 64), reads=BD.k, writes=BD.k)
        S.op("pool", lambda e: e.memset(BD.ap(64, 128)[:, 64:128], 1.0 / 64), reads=BD.k, writes=BD.k)
        S.op("pool", lambda e: e.memset(onesrow.ap(), 1.0), writes=onesrow.k)
        S.op("pool", lambda e: e.memset(fillsrc.ap(), 0.5), writes=fillsrc.k)
        S.op("dve", lambda e: e.tensor_scalar(out=gqs.ap(), in0=gqs.ap(), scalar1=0.125, scalar2=None, op0=ALU.mult),
             reads=gqs.k, writes=gqs.k)
        S.op("dve", lambda e: e.tensor_scalar(out=negfb.ap(0, 16), in0=negfb.ap(0, 16), scalar1=-1.0, scalar2=None,
                                              op0=ALU.mult), reads=negfb.k, writes=negfb.k)
        if "a" in stages or "ia" in stages:
            fi = Bump(FA, FS0, F_ELEMS)
            wsn = fi([8, 128])
            wsTf = fi([8, 128])
            Bmat = fi([1024])
            bsb = fi([8, 128])
            sdma(wsn, wsn.ap(), aws_d.rearrange("g t s -> t g s"))
            sdma(Bmat, Bmat.ap(), alnb_d.partition_broadcast(128))
            sdma(bsb, bsb.ap().rearrange("p a b -> p (a b)"), abs_d.partition_broadcast(128))
            S.op("pool", lambda e: e.memset(wsn.ap(0, 64)[:, :, 64:128], 0.0), reads=wsn.k, writes=wsn.k)
            for g in range(8):
                b = nbank()
                S.op("pe", lambda e, g=g, b=b: e.transpose(out=psf[b][:, 0:128], in_=wsn.ap()[:, g, :],
                                                          identity=identf.ap()),
                     reads=wsn.k + identf.k, writes=PK[b])
                S.op("dve", lambda e, g=g, b=b: e.tensor_copy(out=wsTf.ap()[:, g, :], in_=psf[b][:, 0:128]),
                     reads=PK[b], writes=wsTf[g].k)
                S.op("pool", lambda e, g=g: e.tensor_copy(out=wsT.ap()[:, g, :], in_=wsTf.ap()[:, g, :]),
                     reads=wsTf[g].k, writes=wsT[g].k)
                b2 = nbank()
                S.op("pe", lambda e, g=g, b2=b2: e.matmul(psf[b2][:, 0:128], lhsT=Bmat.ap()[:, g * 128:(g + 1) * 128],
                                                          rhs=wsTf.ap()[:, g, :], start=True, stop=True),
                     reads=Bmat.k + wsTf[g].k, writes=PK[b2])
                S.op("dve", lambda e, g=g, b2=b2: e.tensor_tensor(out=Bias.ap()[:, g, :], in0=psf[b2][:, 0:128],
                                                                  in1=bsb.ap()[:, g, :], op=ALU.add),
                     reads=PK[b2] + bsb.k, writes=Bias[g].k)

        def norm_stage(j):
            for i in range(4):
                norm_tile(j, i)

        def norm_tile(j, i):
            if True:
                S.op("pool", lambda e, i=i: e.memset(ms.ap()[:, 4 * i:4 * i + 4], 0.0), writes=ms.k)
                for n in range(4 * i, 4 * i + 4):
                    S.op("act", lambda e, n=n: e.activation(out=junk.ap(), in_=h.ap()[:, n, :], func=AF.Square,
                                                            scale=1.0 / 32, accum_out=ms.ap()[:, n:n + 1]),
                         reads=h[n].k, writes=junk.k + ms.k)
                S.op("dve", lambda e, i=i: e.tensor_scalar(out=mse.ap()[:, 4 * i:4 * i + 4], in0=ms.ap()[:, 4 * i:4 * i + 4],
                                                           scalar1=EPS, scalar2=None, op0=ALU.add),
                     reads=ms.k, writes=mse.k)
                S.op("act", lambda e, i=i: e.activation(out=mse.ap()[:, 4 * i:4 * i + 4], in_=mse.ap()[:, 4 * i:4 * i + 4],
                                                        func=AF.Sqrt), reads=mse.k, writes=mse.k)
                S.op("dve", lambda e, i=i: e.reciprocal(out=rstd.ap()[:, 4 * i:4 * i + 4], in_=mse.ap()[:, 4 * i:4 * i + 4]),
                     reads=mse.k, writes=rstd.k)
                for n in range(4 * i, 4 * i + 4):
                    hb = hs[n % 2]
                    pb = 0
                    S.op("dve", lambda e, n=n, hb=hb: e.tensor_scalar(out=hb.ap(), in0=h.ap()[:, n, :],
                                                                       scalar1=rstd.ap()[:, n:n + 1], scalar2=None,
                                                                       op0=ALU.mult),
                         reads=h[n].k + rstd.k, writes=hb.k)
                    for c in range(8):
                        S.op("pe", lambda e, c=c, hb=hb, pb=pb: e.transpose(out=psb[pb][:, c * 128:(c + 1) * 128],
                                                                           in_=hb.ap()[:, c * 128:(c + 1) * 128],
                                                                           identity=identb.ap()),
                             reads=hb.k + identb.k, writes=PBK[pb])
                    wk = []
                    for c in range(8):
                        wk += hnT[c].cols(n * 128, (n + 1) * 128).k
                    S.op("dve", lambda e, n=n, pb=pb: e.tensor_tensor(
                        out=hnT.ap()[:, :, n * 128:(n + 1) * 128],
                        in0=psb[pb][:, :].rearrange("p (c t) -> p c t", t=128),
                        in1=gl.ap()[:, j * 8:(j + 1) * 8].unsqueeze(2).to_broadcast([128, 8, 128]),
                        op=ALU.mult), reads=PBK[pb] + gl.k, writes=wk)

        def hn_tile(kc, i):
            return hnT[kc].cols(i * 512, (i + 1) * 512)

        def hn_sub(kc, n):
            return hnT[kc].cols(n * 128, (n + 1) * 128)

        def resid_add(bank, n, nb):
            hv = h[n].cols(nb * 512, (nb + 1) * 512)
            S.op("dve", lambda e: e.tensor_tensor(out=hv.ap(), in0=psf[bank][:, :], in1=hv.ap(), op=ALU.add),
                 reads=PK[bank] + hv.k, writes=hv.k)

        def gmlp_stage(on_final):
            bb = Bump(BA, BS0, B_ELEMS)
            winU = bb([8, 1024])
            winV = bb([8, 1024])
            wout = bb([8, 1024])
            uT = bb([8, 512])
            vn = bb([4, 1024])
            fs = Bump(FA, FS0, F_ELEMS)
            vg = [fs([1024]), fs([1024])]
            sptmp = [fs([512]), fs([512])]
            lnst = [(fs([2, 6]), fs([2]), fs([1]), fs([1])) for _ in range(2)]
            a_in = awin_d.rearrange("(kc p) n -> p kc n", p=128)
            for q4 in range(4):
                S.op("pool", lambda e, q4=q4: e.dma_start(out=winU.ap()[:, :, q4 * 256:(q4 + 1) * 256],
                                                          in_=a_in[:, :, q4 * 256:(q4 + 1) * 256]),
                     writes=[k_ for kc in range(8) for k_ in winU[kc].cols(q4 * 256, (q4 + 1) * 256).k], dma=True)
            for q2 in range(2):
                S.op("pool", lambda e, q2=q2: e.dma_start(out=winV.ap()[:, :, q2 * 512:(q2 + 1) * 512],
                                                          in_=a_in[:, :, 1024 + q2 * 512:1024 + (q2 + 1) * 512]),
                     writes=[k_ for kc in range(8) for k_ in winV[kc].cols(q2 * 512, (q2 + 1) * 512).k], dma=True)
            wdma(wout, wout.ap(), awout_d.rearrange("(kc p) n -> p kc n", p=128))
            def u_part(i):
                for oc in range(8):
                    b = nbank()
                    for kc in range(8):
                        mm(b, psf[b][:, :], winU.ap()[:, kc, oc * 128:(oc + 1) * 128], hn_tile(kc, i).ap(),
                           kc == 0, kc == 7, winU[kc].cols(oc * 128, (oc + 1) * 128).k + hn_tile(kc, i).k)
                    S.op("act", lambda e, oc=oc, b=b: e.activation(out=uT.ap()[:, oc, :], in_=psf[b][:, :], func=AF.Gelu),
                         reads=PK[b], writes=uT[oc].k)

            def v_part(i):
                pending = None
                for s in range(4):
                    n = 4 * i + s
                    vb = vg[s % 2]
                    st6b, mvb, veb, rsb = lnst[s % 2]
                    for nb in range(2):
                        b = nbank()
                        for kc in range(8):
                            mm(b, psf[b][:, :], hn_sub(kc, n).ap(), winV.ap()[:, kc, nb * 512:(nb + 1) * 512],
                               kc == 0, kc == 7, winV[kc].cols(nb * 512, (nb + 1) * 512).k + hn_sub(kc, n).k)
                        S.op("act", lambda e, nb=nb, b=b, vb=vb: e.activation(out=vb.ap()[:, nb * 512:(nb + 1) * 512],
                                                                             in_=psf[b][:, :], func=AF.Gelu),
                             reads=PK[b], writes=vb.cols(nb * 512, (nb + 1) * 512).k)
                        S.op("dve", lambda e, nb=nb, vb=vb, st6b=st6b: e.bn_stats(out=st6b.ap()[:, nb, :],
                                                                                 in_=vb.ap()[:, nb * 512:(nb + 1) * 512]),
                             reads=vb.cols(nb * 512, (nb + 1) * 512).k, writes=st6b.k)
                    S.op("dve", lambda e, st6b=st6b, mvb=mvb: e.bn_aggr(out=mvb.ap(), in_=st6b.ap().rearrange("p a b -> p (a b)")),
                         reads=st6b.k, writes=mvb.k)
                    S.op("dve", lambda e, mvb=mvb, veb=veb: e.tensor_scalar(out=veb.ap(), in0=mvb.ap()[:, 1:2], scalar1=EPS,
                                                                         scalar2=None, op0=ALU.add), reads=mvb.k, writes=veb.k)

                    def tail(s=s, vb=vb, mvb=mvb, veb=veb, rsb=rsb):
                        S.op("act", lambda e: e.activation(out=veb.ap(), in_=veb.ap(), func=AF.Sqrt), reads=veb.k, writes=veb.k)
                        S.op("dve", lambda e: e.reciprocal(out=rsb.ap(), in_=veb.ap()), reads=veb.k, writes=rsb.k)
                        S.op("dve", lambda e: e.tensor_scalar(out=vn.ap()[:, s, :], in0=vb.ap(),
                                                              scalar1=mvb.ap()[:, 0:1], scalar2=rsb.ap(),
                                                              op0=ALU.subtract, op1=ALU.mult),
                             reads=vb.k + mvb.k + rsb.k, writes=vn[s].k)
                    if pending is not None:
                        pending()
                    pending = tail
                pending()

            def sp_part(i):
                for g in range(8):
                    b = nbank()
                    for s in range(4):
                        mm(b, psf[b][:, s * 128:(s + 1) * 128], vn.ap()[:, s, g * 128:(g + 1) * 128], wsT.ap()[:, g, :],
                           True, True, vn[s].k + wsT[g].k)
                    sp_ = sptmp[g % 2]
                    S.op("dve", lambda e, g=g, b=b, sp_=sp_: e.scalar_tensor_tensor(
                        out=sp_.ap().rearrange("p (s t) -> p s t", t=128),
                        in0=psf[b][:, :].rearrange("p (s t) -> p s t", t=128),
                        scalar=alng.ap()[:, g:g + 1],
                        in1=Bias.ap()[:, g, :].unsqueeze(1).to_broadcast([128, 4, 128]),
                        op0=ALU.mult, op1=ALU.add), reads=PK[b] + alng.k + Bias[g].k, writes=sp_.k)
                    S.op("pool", lambda e, g=g, sp_=sp_: e.tensor_tensor(out=uT.ap()[:, g, :], in0=sp_.ap(),
                                                                         in1=uT.ap()[:, g, :], op=ALU.mult),
                         reads=sp_.k + uT[g].k, writes=uT[g].k)

            def out_part(i):
                for s in range(4):
                    n = 4 * i + s
                    for nb in range(2):
                        b = nbank()
                        for kc in range(8):
                            mm(b, psf[b][:, :], uT.ap()[:, kc, s * 128:(s + 1) * 128],
                               wout.ap()[:, kc, nb * 512:(nb + 1) * 512], kc == 0, kc == 7, uT[kc].k + wout[kc].k)
                        resid_add(b, n, nb)

            u_part(0)
            v_part(0)
            sp_part(0)
            for i in range(1, 4):
                v_part(i)
                out_part(i - 1)
                on_final(i - 1)
                u_part(i)
                sp_part(i)
            out_part(3)
            on_final(3)

        def ffn_stage(l, on_final):
            bb = Bump(BA, BS0, B_ELEMS)
            slots = []
            for _ in range(2):
                slots.append((bb([8, 512]), bb([8, 512]), bb([4, 1024])))
            hid = [bb([4, 512]), bb([4, 512])]
            fs = Bump(FA, FS0, F_ELEMS)
            sg = [fs([512]), fs([512])]
            g_in = wg_d[l].rearrange("(kc p) n -> p kc n", p=128)
            u_in = wu_d[l].rearrange("(kc p) n -> p kc n", p=128)
            d_in = wd_d[l].rearrange("(f p) n -> p f n", p=128)

            def load(gi):
                c0, nch = FFN_GROUPS[gi]
                wgv, wuv, wdv = slots[gi % 2]
                wdma(wgv, wgv.ap()[:, :, 0:nch * 128], g_in[:, :, c0 * 128:(c0 + nch) * 128])
                wdma(wuv, wuv.ap()[:, :, 0:nch * 128], u_in[:, :, c0 * 128:(c0 + nch) * 128])
                wdma(wdv, wdv.ap()[:, 0:nch, :], d_in[:, c0:c0 + nch, :])

            load(0)
            cnt = 0
            for gi, (c0, nch) in enumerate(FFN_GROUPS):
                if gi + 1 < len(FFN_GROUPS):
                    load(gi + 1)
                wgv, wuv, wdv = slots[gi % 2]
                for i in range(4):
                    hb = hid[cnt % 2]
                    cnt += 1
                    for fl in range(nch):
                        bg = nbank()
                        for kc in range(8):
                            mm(bg, psf[bg][:, :], wgv.ap()[:, kc, fl * 128:(fl + 1) * 128], hn_tile(kc, i).ap(),
                               kc == 0, kc == 7, wgv[kc].k + hn_tile(kc, i).k)
                        bu = nbank()
                        for kc in range(8):
                            mm(bu, psf[bu][:, :], wuv.ap()[:, kc, fl * 128:(fl + 1) * 128], hn_tile(kc, i).ap(),
                               kc == 0, kc == 7, wuv[kc].k + hn_tile(kc, i).k)
                        sgb = sg[fl % 2]
                        S.op("act", lambda e, bg=bg, sgb=sgb: e.activation(out=sgb.ap(), in_=psf[bg][:, :], func=AF.Silu),
                             reads=PK[bg], writes=sgb.k)
                        S.op("dve", lambda e, bu=bu, sgb=sgb, hb=hb, fl=fl: e.tensor_tensor(
                            out=hb.ap()[:, fl, :], in0=psf[bu][:, :], in1=sgb.ap(), op=ALU.mult),
                            reads=PK[bu] + sgb.k, writes=hb[fl].k)
                    for s in range(4):
                        n = 4 * i + s
                        for nb in range(2):
                            b = nbank()
                            for fl in range(nch):
                                mm(b, psf[b][:, :], hb.ap()[:, fl, s * 128:(s + 1) * 128],
                                   wdv.ap()[:, fl, nb * 512:(nb + 1) * 512], fl == 0, fl == nch - 1, hb[fl].k + wdv[fl].k)
                            resid_add(b, n, nb)
                    if gi == len(FFN_GROUPS) - 1:
                        on_final(i)

        def fox_stage(on_final):
            bb = Bump(BA, BS0, B_ELEMS)
            wf = bb([8, 16])
            pw = [tuple([bb([8, 128]) for _ in range(4)] + [bb([1024])]) for _ in range(2)]
            kT = [bb([2, SEQ]), bb([2, SEQ])]
            qT = [bb([2, 512]), bb([2, 512])]
            vaug = [bb([16, 2, 65]), bb([16, 2, 65])]
            PT = [bb([512]) for _ in range(4)]
            gTh = [bb([2, 512]), bb([2, 512])]
            ogT = [bb([512]), bb([512])]
            sq = [bb([512]), bb([512])]
            cq = bb([SEQ])
            fs = Bump(FA, FS0, F_ELEMS)
            rr = [fs([512]), fs([512])]
            me = [fs([512]), fs([512])]
            rec = [fs([512]), fs([512])]
            bsb2 = [fs([512]), fs([512])]
            gt1 = [fs([512]), fs([512])]
            fs2 = Bump(FA, FS0, F_ELEMS)
            esp = fs2([SEQ])
            ncum = fs2([SEQ])
            ones5 = fs2([512])

            w_in = bwin_d.rearrange("(kc p) n -> p kc n", p=128)
            wdma(wf, wf.ap(), w_in[:, :, 4096:4112])

            def loadpair(p):
                sl = pw[p % 2]
                for t in range(4):
                    wdma(sl[t], sl[t].ap(), w_in[:, :, t * 1024 + p * 128:t * 1024 + (p + 1) * 128])
                wdma(sl[4], sl[4].ap(), bwout_d[p * 128:(p + 1) * 128, :])

            loadpair(0)
            for kb_ in kT:
                S.op("pool", lambda e, kb_=kb_: e.memset(kb_.ap(64, 65), 1.0), writes=kb_.k)
            for vb_ in vaug:
                S.op("pool", lambda e, vb_=vb_: e.memset(vb_.ap()[:, :, :, 64:65], 1.0), writes=vb_.k)
            S.op("pool", lambda e: e.memset(ones5.ap(0, 16), 1.0), writes=ones5.k)

            for i in range(4):
                b = nbank()
                for kc in range(8):
                    mm(b, psf[b][0:16, :], wf.ap()[:, kc, :], hn_tile(kc, i).ap(), kc == 0, kc == 7,
                       wf.k + hn_tile(kc, i).k)
                ev = esp.cols(i * 512, (i + 1) * 512)
                S.op("act", lambda e, b=b, ev=ev: e.activation(out=ev.ap(0, 16), in_=psf[b][0:16, :], func=AF.Exp,
                                                               bias=negfb.ap(0, 16), scale=-1.0),
                     reads=PK[b] + negfb.k, writes=ev.k)
                S.op("act", lambda e, ev=ev: e.activation(out=ev.ap(0, 16), in_=ev.ap(0, 16), func=AF.Ln, bias=1.0, scale=1.0),
                     reads=ev.k, writes=ev.k)
                nv = ncum.cols(i * 512, (i + 1) * 512)
                if i == 0:
                    S.op("dve", lambda e, ev=ev, nv=nv: e.tensor_tensor_scan(out=nv.ap(0, 16), data0=ones5.ap(0, 16),
                                                                             data1=ev.ap(0, 16), initial=0.0,
                                                                             op0=ALU.mult, op1=ALU.add),
                         reads=ev.k + ones5.k, writes=nv.k)
                else:
                    pv = ncum.cols(i * 512 - 1, i * 512)
                    S.op("dve", lambda e, ev=ev, nv=nv, pv=pv: e.tensor_tensor_scan(out=nv.ap(0, 16), data0=ones5.ap(0, 16),
                                                                                    data1=ev.ap(0, 16), initial=pv.ap(0, 16),
                                                                                    op0=ALU.mult, op1=ALU.add),
                         reads=ev.k + ones5.k + pv.k, writes=nv.k)
                cv = cq.cols(i * 512, (i + 1) * 512)
                S.op("dve", lambda e, nv=nv, cv=cv: e.tensor_scalar(out=cv.ap(0, 16), in0=nv.ap(0, 16), scalar1=-1.0,
                                                                    scalar2=None, op0=ALU.mult),
                     reads=nv.k, writes=cv.k)
            bt = nbank()
            for n in range(16):
                S.op("pe", lambda e, n=n, bt=bt: e.transpose(out=psf[bt][:, n * 16:(n + 1) * 16],
                                                             in_=ncum.ap(0, 16)[:, n * 128:(n + 1) * 128],
                                                             identity=identf.ap(0, 16)[:, 0:16]),
                     reads=ncum.k + identf.k, writes=PK[bt])
            S.op("dve", lambda e, bt=bt: e.tensor_copy(out=negc.ap().rearrange("p a b -> p (a b)"), in_=psf[bt][:, 0:256]),
                 reads=PK[bt], writes=negc.k)

            obank = [4, 5]
            PROJ = 3
            tickno = [0]
            dseq = [0]
            deferred = []

            def later(n, fn):
                dseq[0] += 1
                deferred.append((tickno[0] + n, dseq[0], fn))
                deferred.sort(key=lambda t: (t[0], t[1]))

            NFILL = 0

            def filler():
                for _ in range(NFILL):
                    S.op("pe", lambda e: e.matmul(psf[6][:, :], lhsT=identb.ap(), rhs=fillsrc.ap(),
                                                  start=True, stop=True), reads=fillsrc.k + identb.k)

            def tick():
                tickno[0] += 1
                while deferred and deferred[0][0] <= tickno[0]:
                    deferred.pop(0)[2]()

            pc = me
            srr = [0]
            pt_rr = [0]

            def sbank():
                x = srr[0] % 3
                srr[0] += 1
                return x

            def qk_norm(b, which, gsc, dst, i, col0):
                sqb = sq[which]
                pcb = pc[which]
                rrb = rr[which]
                S.op("dve", lambda e: e.tensor_copy(out=pcb.ap(), in_=psf[b][:, :]), reads=PK[b], writes=pcb.k)
                S.op("pool", lambda e: e.tensor_tensor(out=sqb.ap(), in0=pcb.ap(), in1=pcb.ap(), op=ALU.mult),
                     reads=pcb.k, writes=sqb.k)

                def link2():
                    b2 = sbank()
                    mm(b2, psf[b2][:, :], BD.ap(), sqb.ap(), True, True, BD.k + sqb.k)
                    S.op("act", lambda e: e.activation(out=rrb.ap(), in_=psf[b2][:, :], func=AF.Ln, bias=epsc.ap(), scale=1.0),
                         reads=PK[b2] + epsc.k, writes=rrb.k)
                    S.op("act", lambda e: e.activation(out=rrb.ap(), in_=rrb.ap(), func=AF.Exp, scale=-0.5),
                         reads=rrb.k, writes=rrb.k)

                def link3():
                    for hh in range(2):
                        dv = dst[hh].cols(col0, col0 + 512)
                        S.op("dve", lambda e, hh=hh, dv=dv: e.scalar_tensor_tensor(
                            out=dv.ap(0, 64), in0=pcb.ap(hh * 64, (hh + 1) * 64),
                            scalar=gsc.ap(hh * 64, (hh + 1) * 64),
                            in1=rrb.ap(hh * 64, (hh + 1) * 64), op0=ALU.mult, op1=ALU.mult),
                            reads=pcb.k + rrb.k + gsc.k, writes=dv.k)

                later(3, link2)
                later(6, link3)

            wq = []
            projrr = [0]
            PROJB = [3, 6]

            def projbank():
                x = PROJB[projrr[0] % 2]
                projrr[0] += 1
                return x

            def kproj(p, i):
                wk = pw[p % 2][1]
                st_ = {}

                def part(k0):
                    if k0 == 0:
                        st_["b"] = projbank()
                    b = st_["b"]
                    for kc in (k0, k0 + 1):
                        mm(b, psf[b][:, :], wk.ap()[:, kc, :], hn_tile(kc, i).ap(), kc == 0, kc == 7, wk.k + hn_tile(kc, i).k)
                    if k0 == 6:
                        qk_norm(b, 0, gks, kT[p % 2], i, i * 512)
                for k0 in (0, 2, 4, 6):
                    wq.append(lambda k0=k0: part(k0))

            def vproj(p, i):
                wv = pw[p % 2][2]
                vb = vaug[p % 2]
                st_ = {}

                def part(s_):
                    if s_ == 0:
                        st_["b"] = projbank()
                    b = st_["b"]
                    n = 4 * i + s_
                    for kc in range(8):
                        mm(b, psf[b][:, s_ * 128:(s_ + 1) * 128], hn_sub(kc, n).ap(), wv.ap()[:, kc, :],
                           kc == 0, kc == 7, wv.k + hn_sub(kc, n).k)
                    if s_ == 3:
                        wkeys = []
                        for t_ in range(4):
                            wkeys += vb[4 * i + t_].k
                        S.op("dve", lambda e: e.tensor_copy(
                            out=vb.ap()[:, 4 * i:4 * i + 4, :, 0:64],
                            in_=psf[b][:, :].rearrange("p (s h d) -> p s h d", h=2, d=64)),
                            reads=PK[b], writes=wkeys)
                for s_ in range(4):
                    wq.append(lambda s_=s_: part(s_))

            def qproj(p, i):
                wq_ = pw[p % 2][0]
                qb = qT[i % 2]
                st_ = {}

                def part(k0):
                    if k0 == 0:
                        st_["b"] = projbank()
                        for hh in range(2):
                            hd = 2 * p + hh
                            S.op("sp", lambda e, hh=hh, hd=hd: e.dma_start(out=qb.ap(64, 65)[:, hh, :],
                                                                         in_=cq.ap(hd, hd + 1)[:, i * 512:(i + 1) * 512]),
                                 reads=cq.k, writes=qb[hh].k, dma=True)
                    b = st_["b"]
                    for kc in (k0, k0 + 1):
                        mm(b, psf[b][:, :], wq_.ap()[:, kc, :], hn_tile(kc, i).ap(), kc == 0, kc == 7,
                           wq_.k + hn_tile(kc, i).k)
                    if k0 == 6:
                        qk_norm(b, 1, gqs, qb, i, 0)
                for k0 in (0, 2, 4, 6):
                    wq.append(lambda k0=k0: part(k0))

            def gate_proj(p, i):
                wgt = pw[p % 2][3]
                st_ = {}

                def part(k0):
                    if k0 == 0:
                        st_["b"] = projbank()
                    b = st_["b"]
                    for kc in (k0, k0 + 1):
                        mm(b, psf[b][:, :], wgt.ap()[:, kc, :], hn_tile(kc, i).ap(), kc == 0, kc == 7,
                           wgt.k + hn_tile(kc, i).k)
                    if k0 == 6:
                        g1 = gt1[i % 2]
                        gth = gTh[i % 2]
                        S.op("act", lambda e: e.activation(out=g1.ap(), in_=psf[b][:, :], func=AF.Exp, scale=-1.0),
                             reads=PK[b], writes=g1.k)
                        S.op("act", lambda e: e.activation(out=g1.ap(), in_=g1.ap(), func=AF.Ln, bias=1.0, scale=1.0),
                             reads=g1.k, writes=g1.k)
                        for hh in range(2):
                            S.op("act", lambda e, hh=hh: e.activation(out=gth.ap(0, 64)[:, hh, :],
                                                                     in_=g1.ap(hh * 64, (hh + 1) * 64), func=AF.Exp,
                                                                     scale=-1.0),
                                 reads=g1.k, writes=gth[hh].k)
                for k0 in (0, 2, 4, 6):
                    wq.append(lambda k0=k0: part(k0))

            def finish_head(p, i, hh, og, gap):
                ob = obank[hh]
                gth = gTh[i % 2]
                S.op("dve", lambda e: e.reciprocal(out=rec[hh].ap(64, 65), in_=psf[ob][64:65, :]),
                     reads=PK[ob], writes=rec[hh].k)

                def link2():
                    b = sbank()
                    mm(b, psf[b][0:64, :], onesrow.ap(64, 65), rec[hh].ap(64, 65), True, True, onesrow.k + rec[hh].k)
                    bs2 = bsb2[hh]
                    S.op("dve", lambda e: e.tensor_tensor(out=bs2.ap(0, 64), in0=psf[b][0:64, :],
                                                          in1=gth.ap(0, 64)[:, hh, :], op=ALU.mult),
                         reads=PK[b] + gth[hh].k, writes=bs2.k)
                    S.op("dve", lambda e: e.tensor_tensor(out=og.ap(hh * 64, (hh + 1) * 64), in0=psf[ob][0:64, :],
                                                          in1=bs2.ap(0, 64), op=ALU.mult),
                         reads=PK[ob] + bs2.k, writes=og.k)
                    if hh == 1:
                        later(gap, lambda: out_proj(p, i, og))

                later(gap, link2)

            def issue_S(p, i, hh, j):
                hd = 2 * p + hh
                kb = kT[p % 2]
                c0 = max(0, (j - 4 * i) * 128)
                b = sbank()
                kv = kb[hh].cols(j * 128, (j + 1) * 128)
                qv = qT[i % 2][hh].cols(c0, 512)
                mm(b, psf[b][:, c0:512], kv.ap(0, 65), qv.ap(0, 65), True, True, kv.k + qv.k)
                pt = PT[pt_rr[0] % 4]
                pt_rr[0] += 1
                S.op("act", lambda e: e.activation(
                    out=pt.ap()[:, c0:512], in_=psf[b][:, c0:512], func=AF.Exp,
                    bias=negc.ap()[:, j, hd:hd + 1], scale=1.0),
                    reads=PK[b] + negc.k, writes=pt.k)
                if j >= 4 * i:
                    S.op("pool", lambda e: e.affine_select(
                        out=pt.ap()[:, c0:c0 + 128], in_=pt.ap()[:, c0:c0 + 128], pattern=[[1, 128]],
                        compare_op=ALU.is_ge, fill=0.0, base=0, channel_multiplier=-1),
                        reads=pt.k, writes=pt.k)
                return (p, i, hh, j, c0, pt)

            def issue_PV(p, i, hh, j, c0, pt):
                ob = obank[hh]
                vb = vaug[p % 2]
                nk = 4 * i + 4
                mm(ob, psf[ob][0:65, c0:512], vb.ap()[:, j, hh, :], pt.ap()[:, c0:512], j == 0, j == nk - 1,
                   vb[j].k + pt.k)

            def out_proj(p, i, og):
                wo = pw[p % 2][4]

                def part(s_, nb):
                    n = 4 * i + s_
                    b = sbank()
                    mm(b, psf[b][:, :], og.ap()[:, s_ * 128:(s_ + 1) * 128], wo.ap()[:, nb * 512:(nb + 1) * 512],
                       True, True, og.k + wo.k)
                    resid_add(b, n, nb)
                    if s_ == 3 and nb == 1:
                        if p == 7:
                            on_final(i)
                        if i == 3 and p + 2 < 8:
                            loadpair(p + 2)
                for s_ in range(4):
                    for nb in range(2):
                        wq.append(lambda s_=s_, nb=nb: part(s_, nb))

            BUDGET = 2

            def drain(nmax):
                k = 0
                while wq and k < nmax:
                    wq.pop(0)()
                    k += 1
                return k

            loadpair(1)
            for i in range(4):
                kproj(0, i)
                vproj(0, i)
                drain(10 ** 9)
                while deferred:
                    tick()
            qproj(0, 0)
            gate_proj(0, 0)
            drain(10 ** 9)
            while deferred:
                tick()
            LOOK = 2
            pend = []

            def after_pv(p, i, hh, j):
                if j == 4 * i + 3:
                    og = ogT[i % 2]
                    d1, gap = (1, 1) if i == 3 else ((2, 2) if i == 0 else (2, 5))
                    later(d1, lambda: finish_head(p, i, hh, og, gap))

            CHUNK_AT = [20, 27, 34, 41, 48, 55, 62, 69]
            for p in range(8):
                chunks = []
                if p + 1 < 8:
                    for i in range(4):
                        chunks.append(lambda i=i, p=p: kproj(p + 1, i))
                        chunks.append(lambda i=i, p=p: vproj(p + 1, i))
                items = [(i, hh, j) for i in range(4) for hh in range(2) for j in range(4 * i + 4)]
                for idx, (i, hh, j) in enumerate(items):
                    if hh == 0 and j == 0 and i + 1 < 4:
                        qproj(p, i + 1)
                    if hh == 1 and j == 0 and i + 1 < 4:
                        gate_proj(p, i + 1)
                    if i == 3 and hh == 1 and j == 2 and p + 1 < 8:
                        qproj(p + 1, 0)
                        gate_proj(p + 1, 0)
                    if chunks and idx in CHUNK_AT:
                        chunks.pop(0)()
                    pend.append(issue_S(p, i, hh, j))
                    if len(pend) > LOOK:
                        it = pend.pop(0)
                        issue_PV(*it)
                        after_pv(it[0], it[1], it[2], it[3])
                    tick()
                    if drain(BUDGET) == 0:
                        filler()
                while chunks:
                    chunks.pop(0)()
            while pend:
                it = pend.pop(0)
                issue_PV(*it)
                after_pv(it[0], it[1], it[2], it[3])
                tick()
                drain(BUDGET)
            while deferred or wq:
                tick()
                drain(BUDGET)

        GJ = {"a": 0, "f0": 1, "b": 2, "f1": 3}
        real = [st_ for st_ in stages if st_ in GJ]

        def hkeys(i):
            hk = []
            for n in range(4 * i, 4 * i + 4):
                hk += h[n].k
            return hk

        def load_x(sq_i, i):
            S.op("sp", lambda e: e.dma_start(
                out=h.ap()[:, 4 * i:4 * i + 4, :],
                in_=x_d[sq_i, i * 512:(i + 1) * 512, :].rearrange("(n p) d -> p n d", p=128)),
                writes=hkeys(i), dma=True)

        def store_out(sq_i, i):
            S.op("sp", lambda e: e.dma_start(
                out=out_d[sq_i, i * 512:(i + 1) * 512, :].rearrange("(n p) d -> p n d", p=128),
                in_=h.ap()[:, 4 * i:4 * i + 4, :]), reads=hkeys(i), dma=True)

        for i in range(4):
            load_x(0, i)
        if real:
            norm_stage(GJ[real[0]])
        for sq_i in range(nseq):
            for si, stg in enumerate(real):
                last = si == len(real) - 1
                if not last:
                    nj = GJ[real[si + 1]]
                    cb = lambda i, nj=nj: norm_tile(nj, i)
                else:
                    def cb(i, sq_i=sq_i):
                        store_out(sq_i, i)
                        if i == 3 and sq_i + 1 < nseq:
                            for t in range(4):
                                load_x(sq_i + 1, t)
                            for t in range(4):
                                norm_tile(GJ[real[0]], t)
                if stg == "a":
                    gmlp_stage(cb)
                elif stg == "f0":
                    ffn_stage(0, cb)
                elif stg == "b":
                    fox_stage(cb)
                elif stg == "f1":
                    ffn_stage(1, cb)
            if not real:
                for i in range(4):
                    store_out(sq_i, i)
                    if sq_i + 1 < nseq:
                        load_x(sq_i + 1, i)
        with nc.allow_low_precision(reason="bf16 matmul operands by design; fp32 accumulation"):
            S.emit(nc)
    return nc


def host_layout(inputs, seq_slice):
    f = lambda a: np.ascontiguousarray(np.asarray(a, dtype=np.float32))
    gains = np.stack([np.asarray(inputs["norm_mix_g"])[0], np.asarray(inputs["norm_ffn_g"])[0],
                      np.asarray(inputs["norm_mix_g"])[1], np.asarray(inputs["norm_ffn_g"])[1]])
    gl = gains.reshape(4, 8, 128).transpose(2, 0, 1).reshape(128, 32)
    m = {
        "x": f(np.asarray(inputs["x"])[seq_slice]),
        "gl": f(gl),
        "alng": f(np.asarray(inputs["a_ln_g"])[0].reshape(8, 128).T),
        "alnb": f(np.asarray(inputs["a_ln_b"])[0].reshape(1, D)),
        "aws": f(np.asarray(inputs["a_w_s"])[0]),
        "abs": f(np.asarray(inputs["a_b_s"])[0].reshape(1, 1024)),
        "fb": f(np.asarray(inputs["b_f_bias"])[0].reshape(16, 1)),
        "gq": f(np.tile(np.asarray(inputs["b_q_norm_g"])[0], 2).reshape(128, 1)),
        "gk": f(np.tile(np.asarray(inputs["b_k_norm_g"])[0], 2).reshape(128, 1)),
        "a_w_in": f(np.asarray(inputs["a_w_in"])[0]),
        "a_w_out": f(np.asarray(inputs["a_w_out"])[0]),
        "b_w_in": f(np.asarray(inputs["b_w_in"])[0]),
        "b_w_out": f(np.asarray(inputs["b_w_out"])[0]),
        "ffn_w_gate": f(inputs["ffn_w_gate"]),
        "ffn_w_up": f(inputs["ffn_w_up"]),
        "ffn_w_down": f(inputs["ffn_w_down"]),
    }
    return m


_NC_CACHE = {}


def kernel(**inputs):
    if "full" not in _NC_CACHE:
        _NC_CACHE["full"] = build_program(nseq=2)
    nc = _NC_CACHE["full"]
    base = host_layout(inputs, slice(0, 2))
    xs = np.asarray(inputs["x"], dtype=np.float32)
    in_maps = []
    for c in range(N_CORES):
        m = dict(base)
        m["x"] = np.ascontiguousarray(xs[2 * c:2 * c + 2])
        in_maps.append(m)
    res = run_bass_kernel_spmd(nc, in_maps, core_ids=list(range(N_CORES)))
    out = np.concatenate([np.asarray(r["out"], dtype=np.float32) for r in res.results], axis=0)
    return out
```

```python
import contextlib
import numpy as np
import concourse.bass as bass
import concourse.mybir as mybir
from concourse.bass_utils import run_bass_kernel_spmd

F32 = mybir.dt.float32
BF16 = mybir.dt.bfloat16
AF = mybir.ActivationFunctionType
ALU = mybir.AluOpType

ENGS = ("pe", "act", "dve", "pool", "sp")
N_DMA_SEMS = 4
SEQ = 2048
D = 1024
DFF = 2816
NH = 16
EPS = 1e-6
N_CORES = 8


class _Op:
    __slots__ = ("eng", "fn", "deps", "is_dma", "needs_inc", "count", "dsem")

    def __init__(self, eng, fn, is_dma):
        self.eng = eng
        self.fn = fn
        self.deps = []
        self.is_dma = is_dma
        self.needs_inc = False
        self.count = None
        self.dsem = None


class Sched:
    def __init__(self):
        self.streams = {e: [] for e in ENGS}
        self.last_writer = {}
        self.readers = {}
        self.dma_rr = {e: 0 for e in ENGS}
        self.dma_last = {}

    def op(self, eng, fn, reads=(), writes=(), dma=False):
        o = _Op(eng, fn, dma)
        deps = []
        lw = self.last_writer
        rd = self.readers
        for r in reads:
            w = lw.get(r)
            if w is not None:
                deps.append(w)
        for w_ in writes:
            w = lw.get(w_)
            if w is not None:
                deps.append(w)
            x = rd.get(w_)
            if x:
                deps.extend(x)
        if dma:
            slot = self.dma_rr[eng] % N_DMA_SEMS
            self.dma_rr[eng] += 1
            o.dsem = (eng, slot)
            prev = self.dma_last.get(o.dsem)
            if prev is not None:
                deps.append(prev)
            self.dma_last[o.dsem] = o
            o.needs_inc = True
        seen = set()
        for d in deps:
            if d is o or id(d) in seen:
                continue
            seen.add(id(d))
            if d.eng == "pe" and eng == "pe" and not d.is_dma and not dma:
                continue
            d.needs_inc = True
            o.deps.append(d)
        for r in reads:
            l = rd.get(r)
            if l is None:
                rd[r] = [o]
            else:
                l.append(o)
        for w_ in writes:
            lw[w_] = o
            rd[w_] = []
        self.streams[eng].append(o)
        return o

    def emit(self, nc):
        with contextlib.ExitStack() as st:
            sems = {}
            for e in ENGS:
                sems[e] = st.enter_context(nc.semaphore("s_" + e))
                for k in range(N_DMA_SEMS):
                    sems[(e, k)] = st.enter_context(nc.semaphore("d_%s%d" % (e, k)))
            for e in ENGS:
                c = 0
                dc = [0] * N_DMA_SEMS
                for o in self.streams[e]:
                    if o.is_dma:
                        dc[o.dsem[1]] += 16
                        o.count = dc[o.dsem[1]]
                    elif o.needs_inc:
                        c += 1
                        o.count = c
            block = st.enter_context(nc.Block())
            streams = self.streams

            def run(engname, engobj):
                waited = {}
                for o in streams[engname]:
                    need = {}
                    for d in o.deps:
                        key = d.dsem if d.is_dma else d.eng
                        if d.count > need.get(key, 0):
                            need[key] = d.count
                    for key, v in need.items():
                        if v > waited.get(key, 0):
                            engobj.wait_ge(sems[key], v)
                            waited[key] = v
                    ins = o.fn(engobj)
                    if o.is_dma:
                        ins.then_inc(sems[o.dsem], 16)
                    elif o.needs_inc:
                        ins.then_inc(sems[engname], 1)
                last = {}
                for o in streams[engname]:
                    if o.is_dma:
                        last[o.dsem] = o.count
                for key, v in last.items():
                    if v > waited.get(key, 0):
                        engobj.wait_ge(sems[key], v)

            @block.tensor
            def _(e):
                run("pe", e)

            @block.scalar
            def _(e):
                run("act", e)

            @block.vector
            def _(e):
                run("dve", e)

            @block.gpsimd
            def _(e):
                run("pool", e)

            @block.sync
            def _(e):
                run("sp", e)


GR = 256


class Arena:
    def __init__(self, name, t, dsize, nel):
        self.name = name
        self.t = t
        self.dsize = dsize
        self.nel = nel

    def keys(self, off, n):
        b0 = (off * self.dsize) // GR
        b1 = ((off + n) * self.dsize - 1) // GR
        nm = self.name
        return [(nm, b) for b in range(b0, b1 + 1)]


class V:
    def __init__(self, ar, off, shape):
        self.ar = ar
        self.off = off
        self.shape = list(shape)
        n = 1
        for s in shape:
            n *= s
        self.n = n
        assert off + n <= ar.nel, (ar.name, off, n, ar.nel)
        self._k = None

    def ap(self, p0=0, p1=128):
        a = self.ar.t[p0:p1, self.off:self.off + self.n]
        if len(self.shape) == 2:
            a = a.rearrange("p (a b) -> p a b", b=self.shape[1])
        elif len(self.shape) == 3:
            a = a.rearrange("p (a b c) -> p a b c", b=self.shape[1], c=self.shape[2])
        return a

    @property
    def k(self):
        if self._k is None:
            self._k = self.ar.keys(self.off, self.n)
        return self._k

    def __getitem__(self, i):
        sub = self.shape[1:]
        n = 1
        for s in sub:
            n *= s
        return V(self.ar, self.off + i * n, sub if sub else [1])

    def cols(self, a, b):
        assert len(self.shape) == 1
        return V(self.ar, self.off + a, [b - a])


class Bump:
    def __init__(self, ar, off, limit):
        self.ar = ar
        self.off = off
        self.limit = limit

    def __call__(self, shape):
        n = 1
        for s in shape:
            n *= s
        al = GR // self.ar.dsize
        off = (self.off + al - 1) // al * al
        v = V(self.ar, off, shape)
        self.off = off + n
        assert self.off <= self.limit, (self.ar.name, self.off, self.limit)
        return v


F_ELEMS = 24320
B_ELEMS = 56320
FFN_GROUPS = [(0, 4), (4, 4), (8, 4), (12, 4), (16, 3), (19, 3)]


def build_program(nseq=2, stages=("a", "f0", "b", "f1"), debug_out=False):
    nc = bass.Bass("TRN2", target_bir_lowering=False)
    dr = {}

    def din(name, shape):
        dr[name] = nc.dram_tensor(name, list(shape), F32, kind="ExternalInput").ap()
        return dr[name]

    x_d = din("x", [nseq, SEQ, D])
    gl_d = din("gl", [128, 32])
    alng_d = din("alng", [128, 8])
    alnb_d = din("alnb", [1, D])
    aws_d = din("aws", [8, 128, 128])
    abs_d = din("abs", [1, 1024])
    fb_d = din("fb", [16, 1])
    gq_d = din("gq", [128, 1])
    gk_d = din("gk", [128, 1])
    awin_d = din("a_w_in", [D, 2 * D])
    awout_d = din("a_w_out", [D, D])
    bwin_d = din("b_w_in", [D, 4 * D + NH])
    bwout_d = din("b_w_out", [D, D])
    wg_d = din("ffn_w_gate", [2, D, DFF])
    wu_d = din("ffn_w_up", [2, D, DFF])
    wd_d = din("ffn_w_down", [2, DFF, D])
    out_d = nc.dram_tensor("out", [nseq, SEQ, D], F32, kind="ExternalOutput").ap()

    S = Sched()
    st = contextlib.ExitStack()
    with st:
        fa_t = st.enter_context(nc.sbuf_tensor("fa", [128, F_ELEMS], F32))
        ba_t = st.enter_context(nc.sbuf_tensor("ba", [128, B_ELEMS], BF16))
        FA = Arena("f", fa_t, 4, F_ELEMS)
        BA = Arena("b", ba_t, 2, B_ELEMS)
        psf = [st.enter_context(nc.psum_tensor("psf%d" % i, [128, 512], F32)) for i in range(7)]
        psb = [st.enter_context(nc.psum_tensor("psb%d" % i, [128, 1024], BF16)) for i in range(1)]
        PK = [[("ps", i)] for i in range(7)]
        PBK = [[("psb", i)] for i in range(1)]

        fb_ = Bump(FA, 0, F_ELEMS)
        h = fb_([16, 1024])
        Bias = fb_([8, 128])
        gl = fb_([32])
        alng = fb_([8])
        ms = fb_([16])
        mse = fb_([16])
        rstd = fb_([16])
        mh = fb_([64])
        epsc = fb_([1])
        identf = fb_([128])
        gqs = fb_([1])
        gks = fb_([1])
        negfb = fb_([1])
        onesrow = fb_([64])
        negc = fb_([16, 16])
        st6 = fb_([2, 6])
        mv = fb_([2])
        ve = fb_([1])
        rs = fb_([1])
        FS0 = (fb_.off + 63) // 64 * 64
        bb_ = Bump(BA, 0, B_ELEMS)
        hnT = bb_([8, SEQ])
        identb = bb_([128])
        BD = bb_([128])
        wsT = bb_([8, 128])
        hs = [bb_([1024]), bb_([1024])]
        junk = bb_([1024])
        fillsrc = bb_([512])
        BS0 = (bb_.off + 127) // 128 * 128

        bank_rr = [0]

        def nbank():
            b = bank_rr[0] % 6
            bank_rr[0] += 1
            return b

        def mm(bank, out_ap, lhsT, rhs, start, stop, reads):
            S.op("pe", lambda e: e.matmul(out_ap, lhsT=lhsT, rhs=rhs, start=start, stop=stop),
                 reads=reads, writes=PK[bank])

        def wdma(out_v, out_ap, in_ap):
            S.op("pool", lambda e: e.dma_start(out=out_ap, in_=in_ap), writes=out_v.k, dma=True)

        def sdma(v, ap_out, ap_in):
            S.op("sp", lambda e: e.dma_start(out=ap_out, in_=ap_in), writes=v.k, dma=True)

        sdma(gl, gl.ap(), gl_d)
        sdma(alng, alng.ap(), alng_d)
        sdma(gqs, gqs.ap(), gq_d)
        sdma(gks, gks.ap(), gk_d)
        sdma(negfb, negfb.ap(0, 16), fb_d)
        S.op("pool", lambda e: e.memset(mh.ap(), -0.5), writes=mh.k)
        S.op("pool", lambda e: e.memset(epsc.ap(), EPS), writes=epsc.k)
        S.op("pool", lambda e: e.memset(identf.ap(), 0.0), writes=identf.k)
        S.op("pool", lambda e: e.affine_select(out=identf.ap(), in_=identf.ap(), pattern=[[-1, 128]],
                                               compare_op=ALU.not_equal, fill=1.0, base=0, channel_multiplier=1),
             reads=identf.k, writes=identf.k)
        S.op("dve", lambda e: e.tensor_copy(out=identb.ap(), in_=identf.ap()), reads=identf.k, writes=identb.k)
        S.op("pool", lambda e: e.memset(BD.ap(), 0.0), writes=BD.k)
        S.op("pool", lambda e: e.memset(BD.ap(0, 64)[:, 0:64], 1.0 / 64), reads=BD.k, writes=BD.k)
        S.op("pool", lambda e: e.memset(BD.ap(64, 128)[:, 64:128], 1.0 / 64), reads=BD.k, writes=BD.k)
        S.op("pool", lambda e: e.memset(onesrow.ap(), 1.0), writes=onesrow.k)
        S.op("pool", lambda e: e.memset(fillsrc.ap(), 0.5), writes=fillsrc.k)
        S.op("dve", lambda e: e.tensor_scalar(out=gqs.ap(), in0=gqs.ap(), scalar1=0.125, scalar2=None, op0=ALU.mult),
             reads=gqs.k, writes=gqs.k)
        S.op("dve", lambda e: e.tensor_scalar(out=negfb.ap(0, 16), in0=negfb.ap(0, 16), scalar1=-1.0, scalar2=None,
                                              op0=ALU.mult), reads=negfb.k, writes=negfb.k)
        if "a" in stages or "ia" in stages:
            fi = Bump(FA, FS0, F_ELEMS)
            wsn = fi([8, 128])
            wsTf = fi([8, 128])
            Bmat = fi([1024])
            bsb = fi([8, 128])
            sdma(wsn, wsn.ap(), aws_d.rearrange("g t s -> t g s"))
            sdma(Bmat, Bmat.ap(), alnb_d.partition_broadcast(128))
            sdma(bsb, bsb.ap().rearrange("p a b -> p (a b)"), abs_d.partition_broadcast(128))
            S.op("pool", lambda e: e.memset(wsn.ap(0, 64)[:, :, 64:128], 0.0), reads=wsn.k, writes=wsn.k)
            for g in range(8):
                b = nbank()
                S.op("pe", lambda e, g=g, b=b: e.transpose(out=psf[b][:, 0:128], in_=wsn.ap()[:, g, :],
                                                          identity=identf.ap()),
                     reads=wsn.k + identf.k, writes=PK[b])
                S.op("dve", lambda e, g=g, b=b: e.tensor_copy(out=wsTf.ap()[:, g, :], in_=psf[b][:, 0:128]),
                     reads=PK[b], writes=wsTf[g].k)
                S.op("pool", lambda e, g=g: e.tensor_copy(out=wsT.ap()[:, g, :], in_=wsTf.ap()[:, g, :]),
                     reads=wsTf[g].k, writes=wsT[g].k)
                b2 = nbank()
                S.op("pe", lambda e, g=g, b2=b2: e.matmul(psf[b2][:, 0:128], lhsT=Bmat.ap()[:, g * 128:(g + 1) * 128],
                                                          rhs=wsTf.ap()[:, g, :], start=True, stop=True),
                     reads=Bmat.k + wsTf[g].k, writes=PK[b2])
                S.op("dve", lambda e, g=g, b2=b2: e.tensor_tensor(out=Bias.ap()[:, g, :], in0=psf[b2][:, 0:128],
                                                                  in1=bsb.ap()[:, g, :], op=ALU.add),
                     reads=PK[b2] + bsb.k, writes=Bias[g].k)

        def norm_stage(j):
            for i in range(4):
                norm_tile(j, i)

        def norm_tile(j, i):
            if True:
                S.op("pool", lambda e, i=i: e.memset(ms.ap()[:, 4 * i:4 * i + 4], 0.0), writes=ms.k)
                for n in range(4 * i, 4 * i + 4):
                    S.op("act", lambda e, n=n: e.activation(out=junk.ap(), in_=h.ap()[:, n, :], func=AF.Square,
                                                            scale=1.0 / 32, accum_out=ms.ap()[:, n:n + 1]),
                         reads=h[n].k, writes=junk.k + ms.k)
                S.op("dve", lambda e, i=i: e.tensor_scalar(out=mse.ap()[:, 4 * i:4 * i + 4], in0=ms.ap()[:, 4 * i:4 * i + 4],
                                                           scalar1=EPS, scalar2=None, op0=ALU.add),
                     reads=ms.k, writes=mse.k)
                S.op("act", lambda e, i=i: e.activation(out=mse.ap()[:, 4 * i:4 * i + 4], in_=mse.ap()[:, 4 * i:4 * i + 4],
                                                        func=AF.Sqrt), reads=mse.k, writes=mse.k)
                S.op("dve", lambda e, i=i: e.reciprocal(out=rstd.ap()[:, 4 * i:4 * i + 4], in_=mse.ap()[:, 4 * i:4 * i + 4]),
                     reads=mse.k, writes=rstd.k)
                for n in range(4 * i, 4 * i + 4):
                    hb = hs[n % 2]
                    pb = 0
                    S.op("dve", lambda e, n=n, hb=hb: e.tensor_scalar(out=hb.ap(), in0=h.ap()[:, n, :],
                                                                       scalar1=rstd.ap()[:, n:n + 1], scalar2=None,
                                                                       op0=ALU.mult),
                         reads=h[n].k + rstd.k, writes=hb.k)
                    for c in range(8):
                        S.op("pe", lambda e, c=c, hb=hb, pb=pb: e.transpose(out=psb[pb][:, c * 128:(c + 1) * 128],
                                                                           in_=hb.ap()[:, c * 128:(c + 1) * 128],
                                                                           identity=identb.ap()),
                             reads=hb.k + identb.k, writes=PBK[pb])
                    wk = []
                    for c in range(8):
                        wk += hnT[c].cols(n * 128, (n + 1) * 128).k
                    S.op("dve", lambda e, n=n, pb=pb: e.tensor_tensor(
                        out=hnT.ap()[:, :, n * 128:(n + 1) * 128],
                        in0=psb[pb][:, :].rearrange("p (c t) -> p c t", t=128),
                        in1=gl.ap()[:, j * 8:(j + 1) * 8].unsqueeze(2).to_broadcast([128, 8, 128]),
                        op=ALU.mult), reads=PBK[pb] + gl.k, writes=wk)

        def hn_tile(kc, i):
            return hnT[kc].cols(i * 512, (i + 1) * 512)

        def hn_sub(kc, n):
            return hnT[kc].cols(n * 128, (n + 1) * 128)

        def resid_add(bank, n, nb):
            hv = h[n].cols(nb * 512, (nb + 1) * 512)
            S.op("dve", lambda e: e.tensor_tensor(out=hv.ap(), in0=psf[bank][:, :], in1=hv.ap(), op=ALU.add),
                 reads=PK[bank] + hv.k, writes=hv.k)

        def gmlp_stage(on_final):
            bb = Bump(BA, BS0, B_ELEMS)
            winU = bb([8, 1024])
            winV = bb([8, 1024])
            wout = bb([8, 1024])
            uT = bb([8, 512])
            vn = bb([4, 1024])
            fs = Bump(FA, FS0, F_ELEMS)
            vg = [fs([1024]), fs([1024])]
            sptmp = [fs([512]), fs([512])]
            lnst = [(fs([2, 6]), fs([2]), fs([1]), fs([1])) for _ in range(2)]
            a_in = awin_d.rearrange("(kc p) n -> p kc n", p=128)
            for q4 in range(4):
                S.op("pool", lambda e, q4=q4: e.dma_start(out=winU.ap()[:, :, q4 * 256:(q4 + 1) * 256],
                                                          in_=a_in[:, :, q4 * 256:(q4 + 1) * 256]),
                     writes=[k_ for kc in range(8) for k_ in winU[kc].cols(q4 * 256, (q4 + 1) * 256).k], dma=True)
            for q2 in range(2):
                S.op("pool", lambda e, q2=q2: e.dma_start(out=winV.ap()[:, :, q2 * 512:(q2 + 1) * 512],
                                                          in_=a_in[:, :, 1024 + q2 * 512:1024 + (q2 + 1) * 512]),
                     writes=[k_ for kc in range(8) for k_ in winV[kc].cols(q2 * 512, (q2 + 1) * 512).k], dma=True)
            wdma(wout, wout.ap(), awout_d.rearrange("(kc p) n -> p kc n", p=128))
            def u_part(i):
                for oc in range(8):
                    b = nbank()
                    for kc in range(8):
                        mm(b, psf[b][:, :], winU.ap()[:, kc, oc * 128:(oc + 1) * 128], hn_tile(kc, i).ap(),
                           kc == 0, kc == 7, winU[kc].cols(oc * 128, (oc + 1) * 128).k + hn_tile(kc, i).k)
                    S.op("act", lambda e, oc=oc, b=b: e.activation(out=uT.ap()[:, oc, :], in_=psf[b][:, :], func=AF.Gelu),
                         reads=PK[b], writes=uT[oc].k)

            def v_part(i):
                pending = None
                for s in range(4):
                    n = 4 * i + s
                    vb = vg[s % 2]
                    st6b, mvb, veb, rsb = lnst[s % 2]
                    for nb in range(2):
                        b = nbank()
                        for kc in range(8):
                            mm(b, psf[b][:, :], hn_sub(kc, n).ap(), winV.ap()[:, kc, nb * 512:(nb + 1) * 512],
                               kc == 0, kc == 7, winV[kc].cols(nb * 512, (nb + 1) * 512).k + hn_sub(kc, n).k)
                        S.op("act", lambda e, nb=nb, b=b, vb=vb: e.activation(out=vb.ap()[:, nb * 512:(nb + 1) * 512],
                                                                             in_=psf[b][:, :], func=AF.Gelu),
                             reads=PK[b], writes=vb.cols(nb * 512, (nb + 1) * 512).k)
                        S.op("dve", lambda e, nb=nb, vb=vb, st6b=st6b: e.bn_stats(out=st6b.ap()[:, nb, :],
                                                                                 in_=vb.ap()[:, nb * 512:(nb + 1) * 512]),
                             reads=vb.cols(nb * 512, (nb + 1) * 512).k, writes=st6b.k)
                    S.op("dve", lambda e, st6b=st6b, mvb=mvb: e.bn_aggr(out=mvb.ap(), in_=st6b.ap().rearrange("p a b -> p (a b)")),
                         reads=st6b.k, writes=mvb.k)
                    S.op("dve", lambda e, mvb=mvb, veb=veb: e.tensor_scalar(out=veb.ap(), in0=mvb.ap()[:, 1:2], scalar1=EPS,
                                                                         scalar2=None, op0=ALU.add), reads=mvb.k, writes=veb.k)

                    def tail(s=s, vb=vb, mvb=mvb, veb=veb, rsb=rsb):
                        S.op("act", lambda e: e.activation(out=veb.ap(), in_=veb.ap(), func=AF.Sqrt), reads=veb.k, writes=veb.k)
                        S.op("dve", lambda e: e.reciprocal(out=rsb.ap(), in_=veb.ap()), reads=veb.k, writes=rsb.k)
                        S.op("dve", lambda e: e.tensor_scalar(out=vn.ap()[:, s, :], in0=vb.ap(),
                                                              scalar1=mvb.ap()[:, 0:1], scalar2=rsb.ap(),
                                                              op0=ALU.subtract, op1=ALU.mult),
                             reads=vb.k + mvb.k + rsb.k, writes=vn[s].k)
                    if pending is not None:
                        pending()
                    pending = tail
                pending()

            def sp_part(i):
                for g in range(8):
                    b = nbank()
                    for s in range(4):
                        mm(b, psf[b][:, s * 128:(s + 1) * 128], vn.ap()[:, s, g * 128:(g + 1) * 128], wsT.ap()[:, g, :],
                           True, True, vn[s].k + wsT[g].k)
                    sp_ = sptmp[g % 2]
                    S.op("dve", lambda e, g=g, b=b, sp_=sp_: e.scalar_tensor_tensor(
                        out=sp_.ap().rearrange("p (s t) -> p s t", t=128),
                        in0=psf[b][:, :].rearrange("p (s t) -> p s t", t=128),
                        scalar=alng.ap()[:, g:g + 1],
                        in1=Bias.ap()[:, g, :].unsqueeze(1).to_broadcast([128, 4, 128]),
                        op0=ALU.mult, op1=ALU.add), reads=PK[b] + alng.k + Bias[g].k, writes=sp_.k)
                    S.op("pool", lambda e, g=g, sp_=sp_: e.tensor_tensor(out=uT.ap()[:, g, :], in0=sp_.ap(),
                                                                         in1=uT.ap()[:, g, :], op=ALU.mult),
                         reads=sp_.k + uT[g].k, writes=uT[g].k)

            def out_part(i):
                for s in range(4):
                    n = 4 * i + s
                    for nb in range(2):
                        b = nbank()
                        for kc in range(8):
                            mm(b, psf[b][:, :], uT.ap()[:, kc, s * 128:(s + 1) * 128],
                               wout.ap()[:, kc, nb * 512:(nb + 1) * 512], kc == 0, kc == 7, uT[kc].k + wout[kc].k)
                        resid_add(b, n, nb)

            u_part(0)
            v_part(0)
            sp_part(0)
            for i in range(1, 4):
                v_part(i)
                out_part(i - 1)
                on_final(i - 1)
                u_part(i)
                sp_part(i)
            out_part(3)
            on_final(3)

        def ffn_stage(l, on_final):
            bb = Bump(BA, BS0, B_ELEMS)
            slots = []
            for _ in range(2):
                slots.append((bb([8, 512]), bb([8, 512]), bb([4, 1024])))
            hid = [bb([4, 512]), bb([4, 512])]
            fs = Bump(FA, FS0, F_ELEMS)
            sg = [fs([512]), fs([512])]
            g_in = wg_d[l].rearrange("(kc p) n -> p kc n", p=128)
            u_in = wu_d[l].rearrange("(kc p) n -> p kc n", p=128)
            d_in = wd_d[l].rearrange("(f p) n -> p f n", p=128)

            def load(gi):
                c0, nch = FFN_GROUPS[gi]
                wgv, wuv, wdv = slots[gi % 2]
                wdma(wgv, wgv.ap()[:, :, 0:nch * 128], g_in[:, :, c0 * 128:(c0 + nch) * 128])
                wdma(wuv, wuv.ap()[:, :, 0:nch * 128], u_in[:, :, c0 * 128:(c0 + nch) * 128])
                wdma(wdv, wdv.ap()[:, 0:nch, :], d_in[:, c0:c0 + nch, :])

            load(0)
            cnt = 0
            for gi, (c0, nch) in enumerate(FFN_GROUPS):
                if gi + 1 < len(FFN_GROUPS):
                    load(gi + 1)
                wgv, wuv, wdv = slots[gi % 2]
                for i in range(4):
                    hb = hid[cnt % 2]
                    cnt += 1
                    for fl in range(nch):
                        bg = nbank()
                        for kc in range(8):
                            mm(bg, psf[bg][:, :], wgv.ap()[:, kc, fl * 128:(fl + 1) * 128], hn_tile(kc, i).ap(),
                               kc == 0, kc == 7, wgv[kc].k + hn_tile(kc, i).k)
                        bu = nbank()
                        for kc in range(8):
                            mm(bu, psf[bu][:, :], wuv.ap()[:, kc, fl * 128:(fl + 1) * 128], hn_tile(kc, i).ap(),
                               kc == 0, kc == 7, wuv[kc].k + hn_tile(kc, i).k)
                        sgb = sg[fl % 2]
                        S.op("act", lambda e, bg=bg, sgb=sgb: e.activation(out=sgb.ap(), in_=psf[bg][:, :], func=AF.Silu),
                             reads=PK[bg], writes=sgb.k)
                        S.op("dve", lambda e, bu=bu, sgb=sgb, hb=hb, fl=fl: e.tensor_tensor(
                            out=hb.ap()[:, fl, :], in0=psf[bu][:, :], in1=sgb.ap(), op=ALU.mult),
                            reads=PK[bu] + sgb.k, writes=hb[fl].k)
                    for s in range(4):
                        n = 4 * i + s
                        for nb in range(2):
                            b = nbank()
                            for fl in range(nch):
                                mm(b, psf[b][:, :], hb.ap()[:, fl, s * 128:(s + 1) * 128],
                                   wdv.ap()[:, fl, nb * 512:(nb + 1) * 512], fl == 0, fl == nch - 1, hb[fl].k + wdv[fl].k)
                            resid_add(b, n, nb)
                    if gi == len(FFN_GROUPS) - 1:
                        on_final(i)

        def fox_stage(on_final):
            bb = Bump(BA, BS0, B_ELEMS)
            wf = bb([8, 16])
            pw = [tuple([bb([8, 128]) for _ in range(4)] + [bb([1024])]) for _ in range(2)]
            kT = [bb([2, SEQ]), bb([2, SEQ])]
            qT = [bb([2, 512]), bb([2, 512])]
            vaug = [bb([16, 2, 65]), bb([16, 2, 65])]
            PT = [bb([512]) for _ in range(4)]
            gTh = [bb([2, 512]), bb([2, 512])]
            ogT = [bb([512]), bb([512])]
            sq = [bb([512]), bb([512])]
            cq = bb([SEQ])
            fs = Bump(FA, FS0, F_ELEMS)
            rr = [fs([512]), fs([512])]
            me = [fs([512]), fs([512])]
            rec = [fs([512]), fs([512])]
            bsb2 = [fs([512]), fs([512])]
            gt1 = [fs([512]), fs([512])]
            fs2 = Bump(FA, FS0, F_ELEMS)
            esp = fs2([SEQ])
            ncum = fs2([SEQ])
            ones5 = fs2([512])

            w_in = bwin_d.rearrange("(kc p) n -> p kc n", p=128)
            wdma(wf, wf.ap(), w_in[:, :, 4096:4112])

            def loadpair(p):
                sl = pw[p % 2]
                for t in range(4):
                    wdma(sl[t], sl[t].ap(), w_in[:, :, t * 1024 + p * 128:t * 1024 + (p + 1) * 128])
                wdma(sl[4], sl[4].ap(), bwout_d[p * 128:(p + 1) * 128, :])

            loadpair(0)
            for kb_ in kT:
                S.op("pool", lambda e, kb_=kb_: e.memset(kb_.ap(64, 65), 1.0), writes=kb_.k)
            for vb_ in vaug:
                S.op("pool", lambda e, vb_=vb_: e.memset(vb_.ap()[:, :, :, 64:65], 1.0), writes=vb_.k)
            S.op("pool", lambda e: e.memset(ones5.ap(0, 16), 1.0), writes=ones5.k)

            for i in range(4):
                b = nbank()
                for kc in range(8):
                    mm(b, psf[b][0:16, :], wf.ap()[:, kc, :], hn_tile(kc, i).ap(), kc == 0, kc == 7,
                       wf.k + hn_tile(kc, i).k)
                ev = esp.cols(i * 512, (i + 1) * 512)
                S.op("act", lambda e, b=b, ev=ev: e.activation(out=ev.ap(0, 16), in_=psf[b][0:16, :], func=AF.Exp,
                                                               bias=negfb.ap(0, 16), scale=-1.0),
                     reads=PK[b] + negfb.k, writes=ev.k)
                S.op("act", lambda e, ev=ev: e.activation(out=ev.ap(0, 16), in_=ev.ap(0, 16), func=AF.Ln, bias=1.0, scale=1.0),
                     reads=ev.k, writes=ev.k)
                nv = ncum.cols(i * 512, (i + 1) * 512)
                if i == 0:
                    S.op("dve", lambda e, ev=ev, nv=nv: e.tensor_tensor_scan(out=nv.ap(0, 16), data0=ones5.ap(0, 16),
                                                                             data1=ev.ap(0, 16), initial=0.0,
                                                                             op0=ALU.mult, op1=ALU.add),
                         reads=ev.k + ones5.k, writes=nv.k)
                else:
                    pv = ncum.cols(i * 512 - 1, i * 512)
                    S.op("dve", lambda e, ev=ev, nv=nv, pv=pv: e.tensor_tensor_scan(out=nv.ap(0, 16), data0=ones5.ap(0, 16),
                                                                                    data1=ev.ap(0, 16), initial=pv.ap(0, 16),
                                                                                    op0=ALU.mult, op1=ALU.add),
                         reads=ev.k + ones5.k + pv.k, writes=nv.k)
                cv = cq.cols(i * 512, (i + 1) * 512)
                S.op("dve", lambda e, nv=nv, cv=cv: e.tensor_scalar(out=cv.ap(0, 16), in0=nv.ap(0, 16), scalar1=-1.0,
                                                                    scalar2=None, op0=ALU.mult),
                     reads=nv.k, writes=cv.k)
            bt = nbank()
            for n in range(16):
                S.op("pe", lambda e, n=n, bt=bt: e.transpose(out=psf[bt][:, n * 16:(n + 1) * 16],
                                                             in_=ncum.ap(0, 16)[:, n * 128:(n + 1) * 128],
                                                             identity=identf.ap(0, 16)[:, 0:16]),
                     reads=ncum.k + identf.k, writes=PK[bt])
            S.op("dve", lambda e, bt=bt: e.tensor_copy(out=negc.ap().rearrange("p a b -> p (a b)"), in_=psf[bt][:, 0:256]),
                 reads=PK[bt], writes=negc.k)

            obank = [4, 5]
            PROJ = 3
            tickno = [0]
            dseq = [0]
            deferred = []

            def later(n, fn):
                dseq[0] += 1
                deferred.append((tickno[0] + n, dseq[0], fn))
                deferred.sort(key=lambda t: (t[0], t[1]))

            NFILL = 0

            def filler():
                for _ in range(NFILL):
                    S.op("pe", lambda e: e.matmul(psf[6][:, :], lhsT=identb.ap(), rhs=fillsrc.ap(),
                                                  start=True, stop=True), reads=fillsrc.k + identb.k)

            def tick():
                tickno[0] += 1
                while deferred and deferred[0][0] <= tickno[0]:
                    deferred.pop(0)[2]()

            pc = me
            srr = [0]
            pt_rr = [0]

            def sbank():
                x = srr[0] % 3
                srr[0] += 1
                return x

            def qk_norm(b, which, gsc, dst, i, col0):
                sqb = sq[which]
                pcb = pc[which]
                rrb = rr[which]
                S.op("dve", lambda e: e.tensor_copy(out=pcb.ap(), in_=psf[b][:, :]), reads=PK[b], writes=pcb.k)
                S.op("pool", lambda e: e.tensor_tensor(out=sqb.ap(), in0=pcb.ap(), in1=pcb.ap(), op=ALU.mult),
                     reads=pcb.k, writes=sqb.k)

                def link2():
                    b2 = sbank()
                    mm(b2, psf[b2][:, :], BD.ap(), sqb.ap(), True, True, BD.k + sqb.k)
                    S.op("act", lambda e: e.activation(out=rrb.ap(), in_=psf[b2][:, :], func=AF.Ln, bias=epsc.ap(), scale=1.0),
                         reads=PK[b2] + epsc.k, writes=rrb.k)
                    S.op("act", lambda e: e.activation(out=rrb.ap(), in_=rrb.ap(), func=AF.Exp, scale=-0.5),
                         reads=rrb.k, writes=rrb.k)

                def link3():
                    for hh in range(2):
                        dv = dst[hh].cols(col0, col0 + 512)
                        S.op("dve", lambda e, hh=hh, dv=dv: e.scalar_tensor_tensor(
                            out=dv.ap(0, 64), in0=pcb.ap(hh * 64, (hh + 1) * 64),
                            scalar=gsc.ap(hh * 64, (hh + 1) * 64),
                            in1=rrb.ap(hh * 64, (hh + 1) * 64), op0=ALU.mult, op1=ALU.mult),
                            reads=pcb.k + rrb.k + gsc.k, writes=dv.k)

                later(3, link2)
                later(6, link3)

            wq = []
            projrr = [0]
            PROJB = [3, 6]

            def projbank():
                x = PROJB[projrr[0] % 2]
                projrr[0] += 1
                return x

            def kproj(p, i):
                wk = pw[p % 2][1]
                st_ = {}

                def part(k0):
                    if k0 == 0:
                        st_["b"] = projbank()
                    b = st_["b"]
                    for kc in (k0, k0 + 1):
                        mm(b, psf[b][:, :], wk.ap()[:, kc, :], hn_tile(kc, i).ap(), kc == 0, kc == 7, wk.k + hn_tile(kc, i).k)
                    if k0 == 6:
                        qk_norm(b, 0, gks, kT[p % 2], i, i * 512)
                for k0 in (0, 2, 4, 6):
                    wq.append(lambda k0=k0: part(k0))

            def vproj(p, i):
                wv = pw[p % 2][2]
                vb = vaug[p % 2]
                st_ = {}

                def part(s_):
                    if s_ == 0:
                        st_["b"] = projbank()
                    b = st_["b"]
                    n = 4 * i + s_
                    for kc in range(8):
                        mm(b, psf[b][:, s_ * 128:(s_ + 1) * 128], hn_sub(kc, n).ap(), wv.ap()[:, kc, :],
                           kc == 0, kc == 7, wv.k + hn_sub(kc, n).k)
                    if s_ == 3:
                        wkeys = []
                        for t_ in range(4):
                            wkeys += vb[4 * i + t_].k
                        S.op("dve", lambda e: e.tensor_copy(
                            out=vb.ap()[:, 4 * i:4 * i + 4, :, 0:64],
                            in_=psf[b][:, :].rearrange("p (s h d) -> p s h d", h=2, d=64)),
                            reads=PK[b], writes=wkeys)
                for s_ in range(4):
                    wq.append(lambda s_=s_: part(s_))

            def qproj(p, i):
                wq_ = pw[p % 2][0]
                qb = qT[i % 2]
                st_ = {}

                def part(k0):
                    if k0 == 0:
                        st_["b"] = projbank()
                        for hh in range(2):
                            hd = 2 * p + hh
                            S.op("sp", lambda e, hh=hh, hd=hd: e.dma_start(out=qb.ap(64, 65)[:, hh, :],
                                                                         in_=cq.ap(hd, hd + 1)[:, i * 512:(i + 1) * 512]),
                                 reads=cq.k, writes=qb[hh].k, dma=True)
                    b = st_["b"]
                    for kc in (k0, k0 + 1):
                        mm(b, psf[b][:, :], wq_.ap()[:, kc, :], hn_tile(kc, i).ap(), kc == 0, kc == 7,
                           wq_.k + hn_tile(kc, i).k)
                    if k0 == 6:
                        qk_norm(b, 1, gqs, qb, i, 0)
                for k0 in (0, 2, 4, 6):
                    wq.append(lambda k0=k0: part(k0))

            def gate_proj(p, i):
                wgt = pw[p % 2][3]
                st_ = {}

                def part(k0):
                    if k0 == 0:
                        st_["b"] = projbank()
                    b = st_["b"]
                    for kc in (k0, k0 + 1):
                        mm(b, psf[b][:, :], wgt.ap()[:, kc, :], hn_tile(kc, i).ap(), kc == 0, kc == 7,
                           wgt.k + hn_tile(kc, i).k)
                    if k0 == 6:
                        g1 = gt1[i % 2]
                        gth = gTh[i % 2]
                        S.op("act", lambda e: e.activation(out=g1.ap(), in_=psf[b][:, :], func=AF.Exp, scale=-1.0),
                             reads=PK[b], writes=g1.k)
                        S.op("act", lambda e: e.activation(out=g1.ap(), in_=g1.ap(), func=AF.Ln, bias=1.0, scale=1.0),
                             reads=g1.k, writes=g1.k)
                        for hh in range(2):
                            S.op("act", lambda e, hh=hh: e.activation(out=gth.ap(0, 64)[:, hh, :],
                                                                     in_=g1.ap(hh * 64, (hh + 1) * 64), func=AF.Exp,
                                                                     scale=-1.0),
                                 reads=g1.k, writes=gth[hh].k)
                for k0 in (0, 2, 4, 6):
                    wq.append(lambda k0=k0: part(k0))

            def finish_head(p, i, hh, og, gap):
                ob = obank[hh]
                gth = gTh[i % 2]
                S.op("dve", lambda e: e.reciprocal(out=rec[hh].ap(64, 65), in_=psf[ob][64:65, :]),
                     reads=PK[ob], writes=rec[hh].k)

                def link2():
                    b = sbank()
                    mm(b, psf[b][0:64, :], onesrow.ap(64, 65), rec[hh].ap(64, 65), True, True, onesrow.k + rec[hh].k)
                    bs2 = bsb2[hh]
                    S.op("dve", lambda e: e.tensor_tensor(out=bs2.ap(0, 64), in0=psf[b][0:64, :],
                                                          in1=gth.ap(0, 64)[:, hh, :], op=ALU.mult),
                         reads=PK[b] + gth[hh].k, writes=bs2.k)
                    S.op("dve", lambda e: e.tensor_tensor(out=og.ap(hh * 64, (hh + 1) * 64), in0=psf[ob][0:64, :],
                                                          in1=bs2.ap(0, 64), op=ALU.mult),
                         reads=PK[ob] + bs2.k, writes=og.k)
                    if hh == 1:
                        later(gap, lambda: out_proj(p, i, og))

                later(gap, link2)

            def issue_S(p, i, hh, j):
                hd = 2 * p + hh
                kb = kT[p % 2]
                c0 = max(0, (j - 4 * i) * 128)
                b = sbank()
                kv = kb[hh].cols(j * 128, (j + 1) * 128)
                qv = qT[i % 2][hh].cols(c0, 512)
                mm(b, psf[b][:, c0:512], kv.ap(0, 65), qv.ap(0, 65), True, True, kv.k + qv.k)
                pt = PT[pt_rr[0] % 4]
                pt_rr[0] += 1
                S.op("act", lambda e: e.activation(
                    out=pt.ap()[:, c0:512], in_=psf[b][:, c0:512], func=AF.Exp,
                    bias=negc.ap()[:, j, hd:hd + 1], scale=1.0),
                    reads=PK[b] + negc.k, writes=pt.k)
                if j >= 4 * i:
                    S.op("pool", lambda e: e.affine_select(
                        out=pt.ap()[:, c0:c0 + 128], in_=pt.ap()[:, c0:c0 + 128], pattern=[[1, 128]],
                        compare_op=ALU.is_ge, fill=0.0, base=0, channel_multiplier=-1),
                        reads=pt.k, writes=pt.k)
                return (p, i, hh, j, c0, pt)

            def issue_PV(p, i, hh, j, c0, pt):
                ob = obank[hh]
                vb = vaug[p % 2]
                nk = 4 * i + 4
                mm(ob, psf[ob][0:65, c0:512], vb.ap()[:, j, hh, :], pt.ap()[:, c0:512], j == 0, j == nk - 1,
                   vb[j].k + pt.k)

            def out_proj(p, i, og):
                wo = pw[p % 2][4]

                def part(s_, nb):
                    n = 4 * i + s_
                    b = sbank()
                    mm(b, psf[b][:, :], og.ap()[:, s_ * 128:(s_ + 1) * 128], wo.ap()[:, nb * 512:(nb + 1) * 512],
                       True, True, og.k + wo.k)
                    resid_add(b, n, nb)
                    if s_ == 3 and nb == 1:
                        if p == 7:
                            on_final(i)
                        if i == 3 and p + 2 < 8:
                            loadpair(p + 2)
                for s_ in range(4):
                    for nb in range(2):
                        wq.append(lambda s_=s_, nb=nb: part(s_, nb))

            BUDGET = 2

            def drain(nmax):
                k = 0
                while wq and k < nmax:
                    wq.pop(0)()
                    k += 1
                return k

            loadpair(1)
            for i in range(4):
                kproj(0, i)
                vproj(0, i)
                drain(10 ** 9)
                while deferred:
                    tick()
            qproj(0, 0)
            gate_proj(0, 0)
            drain(10 ** 9)
            while deferred:
                tick()
            LOOK = 3
            pend = []

            def after_pv(p, i, hh, j):
                if j == 4 * i + 3:
                    og = ogT[i % 2]
                    d1, gap = (1, 1) if i == 3 else ((2, 2) if i == 0 else (2, 5))
                    later(d1, lambda: finish_head(p, i, hh, og, gap))

            CHUNK_AT = [20, 27, 34, 41, 48, 55, 62, 69]
            for p in range(8):
                chunks = []
                if p + 1 < 8:
                    for i in range(4):
                        chunks.append(lambda i=i, p=p: kproj(p + 1, i))
                        chunks.append(lambda i=i, p=p: vproj(p + 1, i))
                items = [(i, hh, j) for i in range(4) for hh in range(2) for j in range(4 * i + 4)]
                for idx, (i, hh, j) in enumerate(items):
                    if hh == 0 and j == 0 and i + 1 < 4:
                        qproj(p, i + 1)
                    if hh == 1 and j == 0 and i + 1 < 4:
                        gate_proj(p, i + 1)
                    if i == 3 and hh == 1 and j == 2 and p + 1 < 8:
                        qproj(p + 1, 0)
                        gate_proj(p + 1, 0)
                    if chunks and idx in CHUNK_AT:
                        chunks.pop(0)()
                    pend.append(issue_S(p, i, hh, j))
                    if len(pend) > LOOK:
                        it = pend.pop(0)
                        issue_PV(*it)
                        after_pv(it[0], it[1], it[2], it[3])
                    tick()
                    if drain(BUDGET) == 0:
                        filler()
                while chunks:
                    chunks.pop(0)()
            while pend:
                it = pend.pop(0)
                issue_PV(*it)
                after_pv(it[0], it[1], it[2], it[3])
                tick()
                drain(BUDGET)
            while deferred or wq:
                tick()
                drain(BUDGET)

        GJ = {"a": 0, "f0": 1, "b": 2, "f1": 3}
        real = [st_ for st_ in stages if st_ in GJ]

        def hkeys(i):
            hk = []
            for n in range(4 * i, 4 * i + 4):
                hk += h[n].k
            return hk

        def load_x(sq_i, i):
            S.op("sp", lambda e: e.dma_start(
                out=h.ap()[:, 4 * i:4 * i + 4, :],
                in_=x_d[sq_i, i * 512:(i + 1) * 512, :].rearrange("(n p) d -> p n d", p=128)),
                writes=hkeys(i), dma=True)

        def store_out(sq_i, i):
            S.op("sp", lambda e: e.dma_start(
                out=out_d[sq_i, i * 512:(i + 1) * 512, :].rearrange("(n p) d -> p n d", p=128),
                in_=h.ap()[:, 4 * i:4 * i + 4, :]), reads=hkeys(i), dma=True)

        for i in range(4):
            load_x(0, i)
        if real:
            norm_stage(GJ[real[0]])
        for sq_i in range(nseq):
            for si, stg in enumerate(real):
                last = si == len(real) - 1
                if not last:
                    nj = GJ[real[si + 1]]
                    cb = lambda i, nj=nj: norm_tile(nj, i)
                else:
                    def cb(i, sq_i=sq_i):
                        store_out(sq_i, i)
                        if i == 3 and sq_i + 1 < nseq:
                            for t in range(4):
                                load_x(sq_i + 1, t)
                            for t in range(4):
                                norm_tile(GJ[real[0]], t)
                if stg == "a":
                    gmlp_stage(cb)
                elif stg == "f0":
                    ffn_stage(0, cb)
                elif stg == "b":
                    fox_stage(cb)
                elif stg == "f1":
                    ffn_stage(1, cb)
            if not real:
                for i in range(4):
                    store_out(sq_i, i)
                    if sq_i + 1 < nseq:
                        load_x(sq_i + 1, i)
        with nc.allow_low_precision(reason="bf16 matmul operands by design; fp32 accumulation"):
            S.emit(nc)
    return nc


def host_layout(inputs, seq_slice):
    f = lambda a: np.ascontiguousarray(np.asarray(a, dtype=np.float32))
    gains = np.stack([np.asarray(inputs["norm_mix_g"])[0], np.asarray(inputs["norm_ffn_g"])[0],
                      np.asarray(inputs["norm_mix_g"])[1], np.asarray(inputs["norm_ffn_g"])[1]])
    gl = gains.reshape(4, 8, 128).transpose(2, 0, 1).reshape(128, 32)
    m = {
        "x": f(np.asarray(inputs["x"])[seq_slice]),
        "gl": f(gl),
        "alng": f(np.asarray(inputs["a_ln_g"])[0].reshape(8, 128).T),
        "alnb": f(np.asarray(inputs["a_ln_b"])[0].reshape(1, D)),
        "aws": f(np.asarray(inputs["a_w_s"])[0]),
        "abs": f(np.asarray(inputs["a_b_s"])[0].reshape(1, 1024)),
        "fb": f(np.asarray(inputs["b_f_bias"])[0].reshape(16, 1)),
        "gq": f(np.tile(np.asarray(inputs["b_q_norm_g"])[0], 2).reshape(128, 1)),
        "gk": f(np.tile(np.asarray(inputs["b_k_norm_g"])[0], 2).reshape(128, 1)),
        "a_w_in": f(np.asarray(inputs["a_w_in"])[0]),
        "a_w_out": f(np.asarray(inputs["a_w_out"])[0]),
        "b_w_in": f(np.asarray(inputs["b_w_in"])[0]),
        "b_w_out": f(np.asarray(inputs["b_w_out"])[0]),
        "ffn_w_gate": f(inputs["ffn_w_gate"]),
        "ffn_w_up": f(inputs["ffn_w_up"]),
        "ffn_w_down": f(inputs["ffn_w_down"]),
    }
    return m


_NC_CACHE = {}


def kernel(**inputs):
    if "full" not in _NC_CACHE:
        _NC_CACHE["full"] = build_program(nseq=2)
    nc = _NC_CACHE["full"]
    base = host_layout(inputs, slice(0, 2))
    xs = np.asarray(inputs["x"], dtype=np.float32)
    in_maps = []
    for c in range(N_CORES):
        m = dict(base)
        m["x"] = np.ascontiguousarray(xs[2 * c:2 * c + 2])
        in_maps.append(m)
    res = run_bass_kernel_spmd(nc, in_maps, core_ids=list(range(N_CORES)))
    out = np.concatenate([np.asarray(r["out"], dtype=np.float32) for r in res.results], axis=0)
    return out
```
